# Optimizing a Trainium2 kernel written in Bass

```python
import jax, jax.numpy as jnp
from jax import lax
import numpy as np

D_MODEL = 1024
BATCH = 32
SEQ = 2048
DEPTH = 2
DEC_BATCH = 16
DEC_SEQ = 32
PAST_LEN = 4096

CHUNK = 64
D_MIX = D_MODEL
W_A = D_MIX // 2
W_B = D_MIX - W_A
H_A = 8
DH_A = W_A // H_A
A_CHUNK = 128
H_B = 8
CONV_K = 31
CONV_STATE = CONV_K - 1
D_IN = 3 * W_A + 3 * W_B
ALPHA = (2 * DEPTH) ** 0.25
BETA = (8 * DEPTH) ** -0.25
LN_EPS = 1e-5

kernel_name = "hymba_gmlp_conformer_stream_step"


def layer_norm(x, g, b):
    xf = x.astype(jnp.float32)
    mu = jnp.mean(xf, axis=-1, keepdims=True)
    var = jnp.mean(jnp.square(xf - mu), axis=-1, keepdims=True)
    y = (xf - mu) * lax.rsqrt(var + LN_EPS)
    return (y * g.astype(jnp.float32) + b.astype(jnp.float32)).astype(x.dtype)


def spatial_gating(v, ws, bias):
    bsz, L, _ = v.shape
    lc = min(L, A_CHUNK)
    n = L // lc
    mask = jnp.tril(jnp.ones((lc, lc), dtype=bool))
    ws_m = jnp.where(mask[None], ws[:, :lc, :lc], jnp.zeros((), ws.dtype))
    vr = v.reshape(bsz, n, lc, H_A, DH_A)
    s = jnp.einsum('hqk,bnkhc->bnqhc', ws_m, vr)
    s = s + jnp.transpose(bias[:, :lc])[None, None, :, :, None]
    return s.reshape(bsz, L, W_A)


def depthwise_causal_conv(hist, a, w, b):
    full = jnp.concatenate([hist, a], axis=1)
    out = lax.conv_general_dilated(full, w[:, None, :].astype(full.dtype), window_strides=(1,),
                                   padding='VALID', dimension_numbers=('NWC', 'WIO', 'NWC'),
                                   feature_group_count=W_B)
    return out + b, full[:, -CONV_STATE:]


def layer(x, conv_hist, w_in, a_ln_g, a_ln_b, a_ws, a_bias, b_conv_w, b_conv_b,
          b_ln_g, b_ln_b, w_out, post_ln_g, post_ln_b):
    h = jnp.einsum('bsd,de->bse', x, w_in)
    u, v, ga, bv, bg, gb = jnp.split(
        h, [W_A, 2 * W_A, 3 * W_A, 3 * W_A + W_B, 3 * W_A + 2 * W_B], axis=-1)
    u = jax.nn.gelu(u, approximate=False)
    v = layer_norm(jax.nn.gelu(v, approximate=False), a_ln_g, a_ln_b)
    y_a = u * spatial_gating(v, a_ws, a_bias) * jax.nn.silu(ga)
    a = bv * jax.nn.sigmoid(bg)
    c, new_hist = depthwise_causal_conv(conv_hist, a, b_conv_w, b_conv_b)
    y_b = jax.nn.silu(layer_norm(c, b_ln_g, b_ln_b)) * jax.nn.silu(gb)
    y = jnp.einsum('bse,ed->bsd', jnp.concatenate([y_a, y_b], axis=-1), w_out)
    x_new = layer_norm(ALPHA * x + y, post_ln_g, post_ln_b)
    return x_new, new_hist, v


def setup_inputs(seed: int = 0) -> dict:
    key = jax.random.key(seed)
    ks = jax.random.split(key, 16)
    f32 = jnp.float32
    nrm = lambda k, s: jax.random.normal(k, s, dtype=f32)
    return {
        "x_prompt": nrm(ks[0], (BATCH, SEQ, D_MODEL)),
        "x_sample": nrm(ks[1], (DEC_BATCH, DEC_SEQ, D_MODEL)),
        "cache_conv": 0.5 * nrm(ks[2], (DEPTH, DEC_BATCH, CONV_STATE, W_B)),
        "w_in": nrm(ks[3], (DEPTH, D_MODEL, D_IN)) * D_MODEL ** -0.5,
        "a_ln_g": 1.0 + 0.02 * nrm(ks[4], (DEPTH, W_A)),
        "a_ln_b": 0.02 * nrm(ks[5], (DEPTH, W_A)),
        "a_ws": nrm(ks[6], (DEPTH, H_A, A_CHUNK, A_CHUNK)) * (0.5 * A_CHUNK ** -0.5),
        "a_bias": 1.0 + 0.02 * nrm(ks[7], (DEPTH, H_A, A_CHUNK)),
        "b_conv_w": nrm(ks[8], (DEPTH, CONV_K, W_B)) * CONV_K ** -0.5,
        "b_conv_b": 0.02 * nrm(ks[9], (DEPTH, W_B)),
        "b_ln_g": 1.0 + 0.02 * nrm(ks[10], (DEPTH, W_B)),
        "b_ln_b": 0.02 * nrm(ks[11], (DEPTH, W_B)),
        "w_out": nrm(ks[12], (DEPTH, D_MIX, D_MODEL)) * (D_MIX ** -0.5 * BETA),
        "post_ln_g": 1.0 + 0.02 * nrm(ks[13], (DEPTH, D_MODEL)),
        "post_ln_b": 0.02 * nrm(ks[14], (DEPTH, D_MODEL)),
    }


def reference(x_prompt, x_sample, cache_conv, w_in, a_ln_g, a_ln_b, a_ws, a_bias,
              b_conv_w, b_conv_b, b_ln_g, b_ln_b, w_out, post_ln_g, post_ln_b):
    xp = x_prompt
    xs = x_sample
    zero_hist = jnp.zeros((x_prompt.shape[0], CONV_STATE, W_B), dtype=x_prompt.dtype)
    conv_p, conv_s, av_s = [], [], []
    for l in range(DEPTH):
        params = (w_in[l], a_ln_g[l], a_ln_b[l], a_ws[l], a_bias[l], b_conv_w[l], b_conv_b[l],
                  b_ln_g[l], b_ln_b[l], w_out[l], post_ln_g[l], post_ln_b[l])
        xp, hist_p, _ = layer(xp, zero_hist, *params)
        xs, hist_s, v_s = layer(xs, cache_conv[l].astype(xs.dtype), *params)
        conv_p.append(hist_p)
        conv_s.append(hist_s)
        av_s.append(v_s)
    new_conv_prompt = jnp.stack(conv_p)
    new_conv_sample = jnp.stack(conv_s)
    new_av_sample = jnp.stack(av_s)
    return (xp, xs, new_conv_prompt, new_conv_sample, new_av_sample)
```

```python
import contextlib
import numpy as np
import concourse.bass as bass
import concourse.mybir as mybir
from concourse.bass_utils import run_bass_kernel_spmd

F32 = mybir.dt.float32
BF16 = mybir.dt.bfloat16
I32 = mybir.dt.int32
AF = mybir.ActivationFunctionType
ALU = mybir.AluOpType

NCORES = 8
D = 1024
DEPTH = 2
SEQ = 2048
BATCH = 32
DEC_BATCH = 16
DEC_SEQ = 32
WA = 512
CONV_K = 31
HIST = 30
ALPHA = float((2 * DEPTH) ** 0.25)
LN_EPS = 1e-5
TT = 512
NSLOT = 4
BLOCKS = [("in", 2048), ("in", 1536), ("in", 2560), ("in", 512), ("in", 1024), ("in", 0),
          ("out", 0), ("out", 512)]
B_BG, B_BV, B_GB, B_V, B_GA, B_U, B_O0, B_O1 = range(8)


class _Eng:
    def __init__(self, name, sem):
        self.name = name
        self.sem = sem
        self.count = 0
        self.ops = []
        self.waited = {}


class Sched:
    def __init__(self):
        self.engs = {}
        self.last_w = {}
        self.readers = {}

    def add_engine(self, name, sem):
        self.engs[name] = _Eng(name, sem)

    def _deps(self, reads, writes):
        deps = {}

        def add(e, v):
            if v > deps.get(e, 0):
                deps[e] = v
        for b in reads:
            lw = self.last_w.get(b)
            if lw:
                add(*lw)
        for b in writes:
            lw = self.last_w.get(b)
            if lw:
                add(*lw)
            for e, v in self.readers.get(b, {}).items():
                add(e, v)
        return deps

    def emit(self, eng, fns, reads=(), writes=(), dma=None):
        E = self.engs[eng]
        if not isinstance(fns, (list, tuple)):
            fns = [fns]
        deps = self._deps(reads, writes)
        done = self.engs[dma] if dma else E
        for e, v in deps.items():
            if e == eng and eng == "pe":
                continue
            if v <= E.waited.get(e, 0):
                continue
            E.waited[e] = v
            E.ops.append(("wait", (e, v)))
        if dma:
            for fn in fns:
                done.count += 16
                E.ops.append(("dma", (fn, dma)))
        else:
            for fn in fns[:-1]:
                E.ops.append(("op", fn))
            done.count += 1
            E.ops.append(("opinc", fns[-1]))
        val = done.count
        dn = done.name
        for b in writes:
            self.last_w[b] = (dn, val)
            self.readers[b] = {}
        for b in reads:
            r = self.readers.setdefault(b, {})
            if val > r.get(dn, 0):
                r[dn] = val

    def settle(self, dma, keys):
        c = self.engs[dma].count
        for k in keys:
            self.last_w[k] = (dma, c)

    def wait_all(self, eng, targets):
        E = self.engs[eng]
        for t in targets:
            T = self.engs[t]
            if T.count > E.waited.get(t, 0):
                E.waited[t] = T.count
                E.ops.append(("wait", (t, T.count)))

    def replay(self, eng, handle):
        E = self.engs[eng]
        for kind, p in E.ops:
            if kind == "wait":
                e, v = p
                handle.wait_ge(self.engs[e].sem, v)
            elif kind == "op":
                p(handle)
            elif kind == "opinc":
                p(handle).then_inc(E.sem, 1)
            else:
                fn, d = p
                fn(handle).then_inc(self.engs[d].sem, 16)


import os
_DBG = set(os.environ.get("KDEBUG", "").split(","))


def build_nc(n_seq=4, seq_len=SEQ, n_samp=2):
    assert seq_len % TT == 0
    nc = bass.Bass("TRN2", target_bir_lowering=False)
    dr = lambda name, shape, dt=F32, kind="ExternalInput": nc.dram_tensor(name, shape, dt, kind=kind).ap()
    xp = dr("xp", [n_seq, seq_len, D])
    xs = dr("xs", [n_samp, DEC_SEQ, D])
    cc = dr("cc", [DEPTH, n_samp, HIST, WA])
    w_in = dr("w_in", [DEPTH, D, 3072])
    w_out = dr("w_out", [DEPTH, D, D])
    a_ln_g = dr("a_ln_g", [DEPTH, WA])
    a_ln_b = dr("a_ln_b", [DEPTH, WA])
    a_ws = dr("a_ws", [DEPTH, 8, 128, 128])
    a_bias = dr("a_bias", [DEPTH, 8, 128])
    b_conv_w = dr("b_conv_w", [DEPTH, CONV_K, WA])
    b_conv_b = dr("b_conv_b", [DEPTH, WA])
    b_ln_g = dr("b_ln_g", [DEPTH, WA])
    b_ln_b = dr("b_ln_b", [DEPTH, WA])
    post_ln_g = dr("post_ln_g", [DEPTH, D])
    post_ln_b = dr("post_ln_b", [DEPTH, D])
    c_ident = dr("c_ident", [128, 128])
    c_tril = dr("c_tril", [128, 128])
    c_i32 = dr("c_i32", [128, 32])
    yp = dr("yp", [n_seq, seq_len, D], kind="ExternalOutput")
    ys = dr("ys", [n_samp, DEC_SEQ, D], kind="ExternalOutput")
    ncp = dr("ncp", [DEPTH, n_seq, HIST, WA], kind="ExternalOutput")
    ncs = dr("ncs", [DEPTH, n_samp, HIST, WA], kind="ExternalOutput")
    nav = dr("nav", [DEPTH, n_samp, DEC_SEQ, WA], kind="ExternalOutput")
    wsc = dr("wsc", [DEPTH, 8, 128, 8 * 512], BF16, kind="Internal")

    es = contextlib.ExitStack()
    with es:
        sb = lambda name, shape, dt=F32: es.enter_context(nc.sbuf_tensor(name, shape, dt))
        S = Sched()
        for n in ["pe", "act", "dve", "pool", "sp", "d_c", "d_w0", "d_w1", "d_w2", "d_w3",
                  "d_s", "d_o0", "d_o1", "d_o2", "d_o3", "d_o4", "d_o5", "d_o6", "d_x5", "d_x6", "d_oc", "d_ov", "d_m", "d_pw0", "d_pw1", "d_pw2", "d_pw3", "d_pm", "d_ws0", "d_ws1", "d_ws2", "d_ws3", "d_x0", "d_x1", "d_x2", "d_x3", "d_x4", "d_m2"]:
            S.add_engine(n, es.enter_context(nc.semaphore("s_" + n)))

        xres = [sb(f"xres{i}", [128, D]) for i in range(7)]
        yT = sb("yT", [128, 8, TT], BF16)
        xbf = yT[:].rearrange("p k t -> p (k t)").rearrange("p (j d) -> p j d", d=D)
        xT = sb("xT", [128, 8, TT], BF16)
        tmpf = [sb(f"tmpf{i}", [128, TT]) for i in range(4)]
        tTt = sb("tT", [128, 4, TT])
        vn = sb("vn", [128, 4, WA], BF16)
        a2T = [sb(f"a2T{l}", [128, 4, 544], BF16) for l in range(DEPTH)]
        Sb = sb("Sb", [128, 16, 540], BF16)
        sgb = sb("sgb", [128, 4, TT])
        cT = sb("cT", [128, 4, TT])
        cbf = sb("cbf", [128, 4, TT], BF16)
        csq = sb("csq", [128, 4, TT], BF16)
        mean_sb = sb("mean_sb", [128, TT])
        rstd_b = sb("rstd_b", [128, TT])
        nt_a = sb("nt_a", [128, TT])
        alast = sb("alast", [128, 4, 2, 32])
        al_p = [sb(f"al_p{i}", [128, 4, 2, 32], BF16) for i in range(3)]
        al_r = sb("al_r", [128, 4, 2, 32])
        vf32 = sb("ostage", [64, WA])
        ostage = vf32
        mhalf = sb("mhalf", [128, 8])
        chbf = sb("chbf", [32, 2, WA], BF16)
        st6 = sb("st6", [128, 8, 6])
        st6v = sb("st6v", [128, 4, 6])
        one1 = sb("one1", [128, 2])
        junk1 = sb("junk1", [128, 2])
        pst = sb("pst", [128, 4, 2, 2])
        psc = sb("psc", [128, 4, 4])
        mv = sb("mv", [128, 4, 2])
        sm = [sb(f"sm{i}", [128, 4]) for i in range(5)]
        wr = [sb(f"wr{i}", [128, 8, 512], BF16) for i in range(NSLOT)]
        Wst = [sb(f"Wst{l}", [128, 16, 8, 32], BF16) for l in range(DEPTH)]
        wTt = sb("wTt", [128, 16, 8])
        WmT = [sb(f"WmT{l}", [128, 8, 128], BF16) for l in range(DEPTH)]
        WmTs = [sb(f"WmTs{l}", [64, 2, 8, 32], BF16) for l in range(DEPTH)]
        Bt = [sb(f"Bt{l}", [128, 4, 128]) for l in range(DEPTH)]
        pg = [sb(f"pg{l}", [128, D]) for l in range(DEPTH)]
        pbb = [sb(f"pbb{l}", [128, D]) for l in range(DEPTH)]
        av_chunk = [5, 6]
        agc = [sb(f"agc{l}", [128, 4]) for l in range(DEPTH)]
        cbc = [sb(f"cbc{l}", [128, 4]) for l in range(DEPTH)]
        bgc = [sb(f"bgc{l}", [128, 4]) for l in range(DEPTH)]
        bbc = [sb(f"bbc{l}", [128, 4]) for l in range(DEPTH)]
        ident_f = sb("ident_f", [128, 128])
        identb = sb("identb", [128, 128], BF16)
        tril_f = sb("tril_f", [128, 128])
        trilb = sb("trilb", [128, 128], BF16)
        i32m = sb("i32m", [128, 32])
        onesb = sb("onesb", [128, 128], BF16)
        wsq = cT[:].rearrange("p m t -> p (m t)")[:, 0:1024].rearrange("p (h k) -> p h k", k=128)
        wsqb = cbf[:].rearrange("p m t -> p (m t)")[:, 0:1024].rearrange("p (h k) -> p h k", k=128)
        vbb = csq[:].rearrange("p m t -> p (m t)")[:, 0:WA]
        abias_bc = sgb[:].rearrange("p m t -> p (m t)")[:, 0:1024].rearrange("p (h q) -> p h q", q=128)

        banks = [es.enter_context(nc.psum_tensor(f"pb{i}", [128, 512], F32)) for i in range(6)]
        ptrs = [es.enter_context(nc.psum_tensor(f"ptr{i}", [128, 1024], BF16)) for i in range(2)]

        class _Ptr:
            def __getitem__(self, idx):
                p, i, c = idx
                return ptrs[i][p, c]
        ptr = _Ptr()
        bank_i = [0]

        def next_bank():
            i = bank_i[0] % 6
            bank_i[0] += 1
            return banks[i], f"pb{i}"
        ptr_i = [0]

        def next_ptr():
            i = ptr_i[0] % 2
            ptr_i[0] += 1
            return i, f"ptr{i}"
        tmp_i = [0]

        def next_tmp():
            i = tmp_i[0] % 4
            tmp_i[0] += 1
            return tmpf[i], f"tmpf{i}"

        def rsqrt_pool(y, x, reads, ykey, shape):
            ex = mhalf[:shape[0], 0:shape[1]]
            S.emit("pool", lambda e: e.tensor_tensor(out=y, in0=x, in1=ex, op=ALU.pow), reads=list(reads) + ["mhalf"], writes=[ykey])

        def rsqrt_dve(y, x, ta, xh, reads, ykey, iters=3, takey=None, xhkey=None):
            keys = [ykey, takey or (ykey + "_ta"), xhkey or (ykey + "_xh")]
            S.emit("dve", lambda e: e.tensor_scalar(out=y.bitcast(I32), in0=x.bitcast(I32), scalar1=1, scalar2=None,
                                                    op0=ALU.arith_shift_right), reads=reads, writes=[keys[0]])
            S.emit("dve", lambda e: e.tensor_scalar(out=y.bitcast(I32), in0=y.bitcast(I32), scalar1=-1,
                                                    scalar2=0x5f3759df, op0=ALU.mult, op1=ALU.add),
                   reads=[keys[0]], writes=[keys[0]])
            S.emit("dve", lambda e: e.tensor_scalar(out=xh, in0=x, scalar1=-0.5, scalar2=None, op0=ALU.mult),
                   reads=reads, writes=[keys[2]])
            for _ in range(iters):
                S.emit("dve", lambda e: e.tensor_tensor(out=ta, in0=y, in1=y, op=ALU.mult), reads=[keys[0]], writes=[keys[1]])
                S.emit("dve", lambda e: e.tensor_tensor(out=ta, in0=ta, in1=xh, op=ALU.mult), reads=[keys[1], keys[2]],
                       writes=[keys[1]])
                S.emit("dve", lambda e: e.scalar_tensor_tensor(out=y, in0=ta, scalar=1.5, in1=y, op0=ALU.add, op1=ALU.mult),
                       reads=[keys[0], keys[1]], writes=[keys[0]])

        CT4 = [f"cT{m}" for m in range(4)]
        SGB4 = [f"sgb{m}" for m in range(4)]
        CBF4 = [f"cbf{m}" for m in range(4)]
        CSQ4 = [f"csq{m}" for m in range(4)]
        pro = []

        def pload(dst_ap, src_ap, key, eng="sp"):
            S.emit(eng, lambda e: e.dma_start(out=dst_ap, in_=src_ap), writes=[key], dma="d_c")
            pro.append(key)
        pload(ident_f[:], c_ident, "ident_f")
        pload(tril_f[:], c_tril, "tril_f")
        pload(i32m[:], c_i32, "i32m")
        for l in range(DEPTH):
            pload(pg[l][:], post_ln_g[l].partition_broadcast(128), f"pg{l}")
            pload(pbb[l][:], post_ln_b[l].partition_broadcast(128), f"pbb{l}")
            pload(xres[av_chunk[l]][:, 512:1024], a_ln_b[l].partition_broadcast(128), f"xres{av_chunk[l]}")
            pload(agc[l][:], a_ln_g[l].rearrange("(m p) -> p m", p=128), f"agc{l}")
            pload(cbc[l][:], b_conv_b[l].rearrange("(m p) -> p m", p=128), f"cbc{l}")
            pload(bgc[l][:], b_ln_g[l].rearrange("(m p) -> p m", p=128), f"bgc{l}")
            pload(bbc[l][:], b_ln_b[l].rearrange("(m p) -> p m", p=128), f"bbc{l}")
        S.settle("d_c", pro)
        S.emit("dve", lambda e: e.tensor_copy(out=identb[:], in_=ident_f[:]), reads=["ident_f"], writes=["identb"])
        S.emit("dve", lambda e: e.tensor_copy(out=trilb[:], in_=tril_f[:]), reads=["tril_f"], writes=["trilb"])
        S.emit("pool", lambda e: e.memset(onesb[:], 1.0 / 512.0), writes=["onesb"])
        S.emit("pool", lambda e: e.memset(mhalf[:], -0.5), writes=["mhalf"])
        S.emit("pool", lambda e: e.memset(one1[:], 1.0), writes=["one1"])
        for l in range(DEPTH):
            S.emit("pool", lambda e, l=l: e.memset(a2T[l][:], 0.0), writes=[f"a2T{l}"])
        S.emit("pool", lambda e: e.memset(Sb[:], 0.0), writes=["Sb"])
        S.emit("pool", lambda e: e.memset(chbf[:], 0.0), writes=["chbf0", "chbf1"])

        for l in range(DEPTH):
            S.emit("sp", lambda e, l=l: e.dma_start(out=wsq, in_=a_ws[l].rearrange("h q k -> q h k")), writes=CT4, dma="d_m")
            S.emit("sp", lambda e, l=l: e.dma_start(out=abias_bc, in_=a_bias[l].rearrange("h q -> (h q)").partition_broadcast(128)
                                                    .rearrange("p (h q) -> p h q", q=128)), writes=SGB4, dma="d_m")
            S.emit("pool", lambda e: e.memset(wTt[:], 0.0), writes=["wTt"])
            cw = []
            for s in range(4):
                for mm in range(8):
                    j = 4 * mm + s
                    if j >= CONV_K:
                        continue
                    cw.append(lambda e, l=l, s=s, mm=mm, j=j: e.dma_start(
                        out=wTt[32 * s:32 * s + 32, :, mm], in_=b_conv_w[l, j, :].rearrange("(g c) -> c g", c=32)))
            S.emit("sp", cw, writes=["wTt"], dma="d_m")
            S.settle("d_m", CT4 + SGB4 + ["wTt"])
            S.emit("dve", lambda e: e.tensor_copy(out=wsqb, in_=wsq), reads=CT4, writes=CBF4)
            for hb in range(2):
                pi, pk = next_ptr()
                S.emit("pe", [(lambda e, h=h, pi=pi: e.transpose(out=ptr[:, pi, 128 * (h % 4):128 * (h % 4) + 128],
                                                                 in_=wsqb[:, h, :], identity=identb[:]))
                              for h in range(4 * hb, 4 * hb + 4)], reads=CBF4 + ["identb"], writes=[pk])
                S.emit("dve", lambda e, l=l, hb=hb, pi=pi: e.tensor_tensor(
                    out=WmT[l][:, 4 * hb:4 * hb + 4, :], in0=ptr[:, pi, 0:512].rearrange("p (h q) -> p h q", q=128),
                    in1=trilb[:].unsqueeze(1).broadcast_to([128, 4, 128]), op=ALU.mult),
                    reads=[pk, "trilb"], writes=[f"WmT{l}"])
            S.emit("pool", lambda e, l=l: e.memset(WmTs[l][:], 0.0), writes=[f"WmTs{l}_0", f"WmTs{l}_1"])
            S.emit("sp", [(lambda e, l=l, s2=s2: e.dma_start(out=WmTs[l][32 * s2:32 * s2 + 32, s2, :, :], in_=WmT[l][0:32, :, 0:32]))
                          for s2 in range(2)], reads=[f"WmT{l}"], writes=[f"WmTs{l}_0", f"WmTs{l}_1"], dma="d_m2")
            S.emit("dve", lambda e, c=av_chunk[l]: e.tensor_copy(out=vbb, in_=xres[c][:, 512:1024]), reads=[f"xres{av_chunk[l]}"], writes=CSQ4)
            pbk, pkk = next_bank()
            S.emit("pe", [(lambda e, l=l, m=m, hh=hh, pbk=pbk: e.matmul(
                pbk[64 * hh:64 * hh + 64, 128 * m:128 * m + 128], lhsT=vbb[:, 64 * (2 * m + hh):64 * (2 * m + hh) + 64],
                rhs=WmT[l][:, 2 * m + hh, :], start=True, stop=True, tile_position=(0, 64 * hh)))
                for m in range(4) for hh in range(2)], reads=CSQ4 + [f"WmT{l}"], writes=[pkk])
            ab4 = abias_bc.rearrange("p (m two) q -> p m two q", two=2)
            for hh in range(2):
                S.emit("dve", lambda e, l=l, hh=hh, pbk=pbk: e.tensor_tensor(
                    out=Bt[l][64 * hh:64 * hh + 64, :, :],
                    in0=pbk[64 * hh:64 * hh + 64, :].rearrange("p (m q) -> p m q", q=128),
                    in1=ab4[64 * hh:64 * hh + 64, :, hh, :], op=ALU.add),
                    reads=[pkk] + SGB4, writes=[f"Bt{l}_{hh}"])
            S.emit("dve", lambda e, l=l: e.scalar_tensor_tensor(
                out=Wst[l][:].rearrange("p g m c -> p (g m) c"),
                in0=wTt[:].rearrange("p g m -> p (g m)").unsqueeze(2).broadcast_to([128, 128, 32]), scalar=0.5,
                in1=i32m[:].unsqueeze(1).broadcast_to([128, 128, 32]), op0=ALU.mult, op1=ALU.mult),
                reads=["wTt", "i32m"], writes=[f"Wst{l}"])

        S.settle("d_m2", [f"WmTs{l}_{s}" for l in range(DEPTH) for s in range(2)])

        gblk = [0]
        wnext = [0]
        plan = []

        def w_src(l, b):
            kind, c0 = BLOCKS[b]
            src = (w_in if kind == "in" else w_out)[l]
            return src[:, c0:c0 + 512].rearrange("(k p) e -> p k e", p=128)

        def w_prefetch(upto):
            while wnext[0] <= upto and wnext[0] < len(plan):
                g = wnext[0]
                l, b, first = plan[g]
                slot = g % NSLOT
                if first:
                    S.emit("pool", lambda e, l=l, b=b, slot=slot: e.dma_start(out=wr[slot][:], in_=w_src(l, b)),
                           writes=[f"wr{slot}"], dma=f"d_pw{slot}")
                    S.emit("sp", lambda e, l=l, b=b, slot=slot: e.dma_start(
                        out=wsc[l, b], in_=wr[slot][:].rearrange("p k e -> p (k e)")),
                        reads=[f"wr{slot}"], writes=[f"wsc{l}_{b}"], dma=f"d_ws{slot}")
                else:
                    S.emit("sp", lambda e, l=l, b=b, slot=slot: e.dma_start(
                        out=wr[slot][:].rearrange("p k e -> p (k e)"), in_=wsc[l, b]),
                        reads=[f"wsc{l}_{b}"], writes=[f"wr{slot}"], dma=f"d_w{slot}")
                wnext[0] += 1

        def w_block(prefetch=True):
            g = gblk[0]
            gblk[0] += 1
            if prefetch:
                w_prefetch(g + NSLOT - 1)
            slot = g % NSLOT
            return wr[slot], f"wr{slot}"

        def tile_layer(l, T, segs, xk, first_in_seq, last_in_seq, sample, hist_src, conv_out, y_out, av_out):
            nj = len(xk)
            PT = min(T, 128)
            nseg = len(segs)
            TS = segs[0][1]
            if sample:
                a2v = a2T[l][:, :, 0:128].rearrange("p m (s c) -> p m s c", c=64)
                Sv = Sb[:, :, 0:120].rearrange("p g (s c) -> p g s c", c=60)
            else:
                a2v = a2T[l][:, :, :].unsqueeze(2)
                Sv = Sb[:, :, :].unsqueeze(2)
            ka2 = f"a2T{l}"
            for j in range(nj):
                S.emit("act", lambda e, j=j: e.copy(out=xbf[:PT, j, :], in_=xk[j][0][:PT, :]),
                       reads=[xk[j][1]], writes=[f"xbf{j}", f"yT{2 * j}", f"yT{2 * j + 1}"])
                pi, pk = next_ptr()
                S.emit("pe", [(lambda e, j=j, k=k, pi=pi: e.transpose(
                    out=ptr[:, pi, 128 * k:128 * k + PT], in_=xbf[:PT, j, 128 * k:128 * k + 128], identity=identb[:PT, :PT]))
                    for k in range(8)], reads=[f"xbf{j}", "identb"], writes=[pk])
                eng = "act" if j % 2 == 1 else "dve"
                src_v = lambda pi=pi: ptr[:, pi, 0:1024].rearrange("p (k t) -> p k t", t=128)[:, :, 0:PT]
                if eng == "act":
                    S.emit("act", lambda e, j=j, src_v=src_v: e.copy(out=xT[:, :, 128 * j:128 * j + PT], in_=src_v()),
                           reads=[pk], writes=[f"xTj{j}"])
                else:
                    S.emit("dve", lambda e, j=j, src_v=src_v: e.tensor_copy(out=xT[:, :, 128 * j:128 * j + PT], in_=src_v()),
                           reads=[pk], writes=[f"xTj{j}"])
            xTk = [f"xTj{j}" for j in range(nj)]

            def fm_block(consume):
                wt, wk = w_block()
                for m in range(4):
                    pb, pk = next_bank()
                    S.emit("pe", [(lambda e, k=k, m=m, pb=pb, wt=wt: e.matmul(
                        pb[:, 0:T], lhsT=wt[:, k, 128 * m:128 * m + 128], rhs=xT[:, k, 0:T], start=(k == 0), stop=(k == 7)))
                        for k in range(8)], reads=[wk] + xTk, writes=[pk])
                    consume(m, pb, pk)

            def fm_block_split(consume):
                wt, wk = w_block()
                jobs = [next_bank() for _ in range(4)]
                for half in range(2):
                    c0 = 256 * half
                    for m in range(4):
                        pb, pk = jobs[m]
                        S.emit("pe", [(lambda e, k=k, m=m, pb=pb, wt=wt, c0=c0: e.matmul(
                            pb[:, c0:c0 + 256], lhsT=wt[:, k, 128 * m:128 * m + 128], rhs=xT[:, k, c0:c0 + 256],
                            start=(k == 0), stop=(k == 7))) for k in range(8)],
                            reads=[wk, f"xTj{2 * half}", f"xTj{2 * half + 1}"], writes=[pk])
                for m in range(4):
                    consume(m, jobs[m][0], jobs[m][1])

            if hist_src is not None:
                for s in range(nseg):
                    S.emit("pool", lambda e, s=s: e.dma_start(out=chbf[0:HIST, s, :], in_=cc[l, hist_src + s]),
                           writes=[f"chbf{s}"], dma="d_pm")
                S.settle("d_pm", [f"chbf{s}" for s in range(nseg)])
                for s in range(nseg):
                    pi, pk = next_ptr()
                    S.emit("pe", [(lambda e, m=m, s=s, pi=pi: e.transpose(
                        out=ptr[:, pi, 32 * m:32 * m + 32], in_=chbf[0:32, s, 128 * m:128 * m + 128],
                        identity=identb[:32, :32])) for m in range(4)], reads=[f"chbf{s}", "identb"], writes=[pk])
                    S.emit("act", lambda e, s=s, pi=pi: e.mul(
                        out=a2v[:, :, s, 0:HIST], in_=ptr[:, pi, 0:128].rearrange("p (m c) -> p m c", c=32)[:, :, 0:HIST], mul=2.0),
                        reads=[pk], writes=[ka2])
            elif first_in_seq:
                S.emit("pool", lambda e: e.memset(a2T[l][:, :, 0:HIST], 0.0), writes=[ka2])
            else:
                S.emit("pool", lambda e: e.tensor_copy(out=a2T[l][:, :, 0:HIST], in_=a2T[l][:, :, TT:TT + HIST]),
                       reads=[ka2], writes=[ka2])

            ths = []

            def c_bg(m, pb, pk):
                tp, tk = next_tmp()
                ths.append((tp, tk))
                S.emit("act", lambda e: e.activation(out=tp[:, 0:T], in_=pb[:, 0:T], func=AF.Tanh, scale=0.5),
                       reads=[pk], writes=[tk])
            (fm_block_split if nj == 4 else fm_block)(c_bg)

            def c_bv(m, pb, pk):
                tp, tk = ths[m]
                fns = [(lambda e, s=s, c0=c0, n=n: e.scalar_tensor_tensor(
                    out=a2v[:, m, s, HIST:HIST + n], in0=tp[:, c0:c0 + n], scalar=1.0, in1=pb[:, c0:c0 + n],
                    op0=ALU.add, op1=ALU.mult)) for s, (c0, n) in enumerate(segs)]
                wr_keys = [ka2]
                if last_in_seq:
                    fns += [(lambda e, s=s, c0=c0, n=n: e.scalar_tensor_tensor(
                        out=alast[:, m, s, :], in0=tp[:, c0 + n - 32:c0 + n], scalar=1.0, in1=pb[:, c0 + n - 32:c0 + n],
                        op0=ALU.add, op1=ALU.mult)) for s, (c0, n) in enumerate(segs)]
                    wr_keys.append("alast")
                S.emit("dve", fns, reads=[tk, pk], writes=wr_keys)
            fm_block(c_bv)

            ncol = TS + 28
            S.emit("sp", [(lambda e, s=s, g=g, sg=sg: e.dma_start(
                out=Sv[32 * s:32 * s + 32, :, sg, 0:ncol].rearrange("p (m f) c -> p m f c", f=4)[:, :, g, :],
                in_=a2v[32 * g:32 * g + 32, :, sg, s:s + ncol])) for s in range(4) for g in range(4) for sg in range(nseg)],
                reads=[ka2], writes=["Sb"], dma="d_s")

            if last_in_seq and "noconvout" not in _DBG:
                S.emit("dve", lambda e: e.tensor_copy(out=al_p[0][:, :, 0:nseg, :], in_=alast[:, :, 0:nseg, :]), reads=["alast"], writes=["al_p0"])
                S.emit("dve", lambda e: e.tensor_tensor(out=al_r[:, :, 0:nseg, :], in0=alast[:, :, 0:nseg, :], in1=al_p[0][:, :, 0:nseg, :],
                                                        op=ALU.subtract), reads=["alast", "al_p0"], writes=["al_r"])
                S.emit("dve", lambda e: e.tensor_copy(out=al_p[1][:, :, 0:nseg, :], in_=al_r[:, :, 0:nseg, :]), reads=["al_r"], writes=["al_p1"])
                S.emit("dve", lambda e: e.tensor_tensor(out=al_r[:, :, 0:nseg, :], in0=al_r[:, :, 0:nseg, :], in1=al_p[1][:, :, 0:nseg, :],
                                                        op=ALU.subtract), reads=["al_r", "al_p1"], writes=["al_r"])
                S.emit("dve", lambda e: e.tensor_copy(out=al_p[2][:, :, 0:nseg, :], in_=al_r[:, :, 0:nseg, :]), reads=["al_r"], writes=["al_p2"])
                for s in range(nseg):
                    pb, pk = next_bank()
                    S.emit("pe", [(lambda e, m=m, i=i, s=s, pb=pb: e.matmul(
                        pb[0:32, 128 * m:128 * m + 128], lhsT=al_p[i][:, m, s, :], rhs=identb[:], start=(i == 0), stop=(i == 2)))
                        for m in range(4) for i in range(3)], reads=["al_p0", "al_p1", "al_p2", "identb"], writes=[pk])
                    S.emit("act", lambda e, pb=pb: e.mul(out=ostage[0:32, :], in_=pb[0:32, :], mul=0.5), reads=[pk], writes=["ostage"])
                    S.emit("sp", lambda e, s=s: e.dma_start(out=conv_out[s], in_=ostage[2:32, :]), reads=["ostage"], dma="d_oc")

            def c_gb(m, pb, pk):
                S.emit("act", lambda e: e.activation(out=sgb[:, m, 0:T], in_=pb[:, 0:T], func=AF.Silu), reads=[pk], writes=[f"sgb{m}"])
            fm_block(c_gb)

            wt, wk = w_block()
            gvs = []
            for j in range(nj):
                pb, pk = next_bank()
                S.emit("pe", [(lambda e, k=k, j=j, pb=pb, wt=wt: e.matmul(
                    pb[:PT, :], lhsT=xT[:, k, 128 * j:128 * j + PT], rhs=wt[:, k, :], start=(k == 0), stop=(k == 7)))
                    for k in range(8)], reads=[wk] + xTk, writes=[pk])
                tp, tk = next_tmp()
                gvs.append((tp, tk))
                S.emit("act", lambda e, pb=pb, tp=tp: e.activation(out=tp[:PT, :], in_=pb[:PT, :], func=AF.Gelu), reads=[pk], writes=[tk])
                S.emit("dve", lambda e, j=j, tp=tp: e.bn_stats(out=st6v[:PT, j, :], in_=tp[:PT, :]), reads=[tk], writes=[f"st6v_{j}"])
                S.emit("dve", lambda e, j=j: e.bn_aggr(out=mv[:PT, j, :], in_=st6v[:PT, j, :]), reads=[f"st6v_{j}"], writes=[f"mv{j}"])
            mvk = [f"mv{j}" for j in range(nj)]
            S.emit("pool", lambda e: e.tensor_scalar(out=sm[0][:PT, 0:nj], in0=mv[:PT, 0:nj, 1], scalar1=LN_EPS, scalar2=None, op0=ALU.add),
                   reads=mvk, writes=["sm0"])
            rsqrt_pool(sm[1][:PT, 0:nj], sm[0][:PT, 0:nj], ["sm0"], "sm1", (PT, nj))
            S.emit("dve", lambda e: e.scalar_tensor_tensor(out=sm[4][:PT, 0:nj], in0=mv[:PT, 0:nj, 0], scalar=-1.0, in1=sm[1][:PT, 0:nj],
                                                           op0=ALU.mult, op1=ALU.mult), reads=mvk + ["sm1"], writes=["sm4"])
            for j in range(nj):
                tp, tk = gvs[j]
                S.emit("act", lambda e, j=j, tp=tp: e.activation(out=vn[:PT, j, :], in_=tp[:PT, :], func=AF.Identity,
                                                                 scale=sm[1][:PT, j:j + 1], bias=sm[4][:PT, j:j + 1]),
                       reads=[tk, "sm1", "sm4"], writes=[f"vn{j}"])
                if sample:
                    S.emit("act", lambda e, j=j, tp=tp: e.activation(out=vf32[:PT, :], in_=tp[:PT, :], func=AF.Identity,
                                                                     scale=sm[1][:PT, j:j + 1], bias=sm[4][:PT, j:j + 1]),
                           reads=[tk, "sm1", "sm4"], writes=["ostage"])
                    avc = av_chunk[l]
                    S.emit("dve", lambda e, avc=avc: e.tensor_tensor(out=vf32[:PT, :], in0=vf32[:PT, :], in1=xres[avc][:PT, 0:512], op=ALU.mult),
                           reads=["ostage", f"xres{avc}"], writes=["ostage"])
                    S.emit("dve", lambda e, avc=avc: e.tensor_tensor(out=vf32[:PT, :], in0=vf32[:PT, :], in1=xres[avc][:PT, 512:1024], op=ALU.add),
                           reads=["ostage", f"xres{avc}"], writes=["ostage"])
                    S.emit("sp", [(lambda e, s=s, c0=c0, n=n: e.dma_start(out=av_out[s], in_=vf32[c0:c0 + n, :]))
                                  for s, (c0, n) in enumerate(segs)], reads=["ostage"], dma="d_ov")

            pbm, pkm = next_bank()
            pbq, pkq = next_bank()

            def stat_mm(m):
                S.emit("pe", lambda e, m=m: e.matmul(pbm[:, 0:T], lhsT=onesb[:], rhs=cbf[:, m, 0:T], start=(m == 0), stop=(m == 3)),
                       reads=["onesb", f"cbf{m}"], writes=[pkm])
                S.emit("pe", lambda e, m=m: e.matmul(pbq[:, 0:T], lhsT=onesb[:], rhs=csq[:, m, 0:T], start=(m == 0), stop=(m == 3)),
                       reads=["onesb", f"csq{m}"], writes=[pkq])

            for m in range(4):
                pb, pk = next_bank()
                fns = []
                for s, (c0, n) in enumerate(segs):
                    for mm in range(8):
                        for g in range(4):
                            fns.append(lambda e, m=m, s=s, c0=c0, n=n, mm=mm, g=g, pb=pb: e.matmul(
                                pb[32 * g:32 * g + 32, c0:c0 + n], lhsT=Wst[l][:, 4 * m + g, mm, :],
                                rhs=Sv[:, 4 * m + g, s, 4 * mm:4 * mm + n], start=(mm == 0), stop=(mm == 7),
                                tile_position=(0, 32 * g)))
                S.emit("pe", fns, reads=["Sb", f"Wst{l}"], writes=[pk])
                S.emit("act", lambda e, m=m, pb=pb: e.activation(out=cbf[:, m, 0:T], in_=pb[:, 0:T], func=AF.Identity,
                                                                 bias=cbc[l][:, m:m + 1], scale=1.0),
                       reads=[pk, f"cbc{l}"], writes=[f"cbf{m}"])
                S.emit("act", lambda e, m=m, pb=pb: e.activation(out=csq[:, m, 0:T], in_=pb[:, 0:T], func=AF.Square,
                                                                 bias=cbc[l][:, m:m + 1], scale=1.0),
                       reads=[pk, f"cbc{l}"], writes=[f"csq{m}"])
                S.emit("act", lambda e, m=m, pb=pb: e.activation(out=cT[:, m, 0:T], in_=pb[:, 0:T], func=AF.Identity,
                                                                 bias=cbc[l][:, m:m + 1], scale=1.0),
                       reads=[pk, f"cbc{l}"], writes=[f"cT{m}"])
                if m >= 1:
                    stat_mm(m - 1)
            stat_mm(3)

            S.emit("act", lambda e: e.copy(out=mean_sb[:, 0:T], in_=pbm[:, 0:T]), reads=[pkm], writes=["mean_sb"])
            S.emit("act", lambda e: e.activation(out=nt_a[:, 0:T], in_=pbm[:, 0:T], func=AF.Square), reads=[pkm], writes=["nt_a"])
            S.emit("act", lambda e: e.activation(out=junk1[:, 0:1], in_=one1[:, 0:1], func=AF.Sqrt), reads=["one1"], writes=["junk1"])
            S.emit("dve", lambda e: e.scalar_tensor_tensor(out=nt_a[:, 0:T], in0=pbq[:, 0:T], scalar=LN_EPS, in1=nt_a[:, 0:T],
                                                           op0=ALU.add, op1=ALU.subtract), reads=[pkq, "nt_a"], writes=["nt_a"])
            tpa, tka = next_tmp()
            tpb, tkb = next_tmp()
            S.emit("act", lambda e: e.activation(out=tpa[:, 0:T], in_=nt_a[:, 0:T], func=AF.Sqrt), reads=["nt_a"], writes=[tka])
            S.emit("act", lambda e: e.activation(out=tpb[:, 0:T], in_=nt_a[:, 0:T], func=AF.Identity, scale=-0.5), reads=["nt_a"], writes=[tkb])
            S.emit("dve", lambda e: e.reciprocal(out=rstd_b[:, 0:T], in_=tpa[:, 0:T]), reads=[tka], writes=["rstd_b"])
            S.emit("dve", lambda e: e.tensor_tensor(out=tpa[:, 0:T], in0=rstd_b[:, 0:T], in1=rstd_b[:, 0:T], op=ALU.mult), reads=["rstd_b"], writes=[tka])
            S.emit("dve", lambda e: e.tensor_tensor(out=tpa[:, 0:T], in0=tpa[:, 0:T], in1=tpb[:, 0:T], op=ALU.mult), reads=[tka, tkb], writes=[tka])
            S.emit("dve", lambda e: e.scalar_tensor_tensor(out=rstd_b[:, 0:T], in0=tpa[:, 0:T], scalar=1.5, in1=rstd_b[:, 0:T],
                                                           op0=ALU.add, op1=ALU.mult), reads=[tka, "rstd_b"], writes=["rstd_b"])
            def c_ga(m, pb, pk):
                S.emit("act", lambda e: e.activation(out=tTt[:, m, 0:T], in_=pb[:, 0:T], func=AF.Silu), reads=[pk], writes=[f"tT{m}"])
            fm_block(c_ga)

            def c_u(m, pb, pk):
                tp, tk = next_tmp()
                S.emit("act", lambda e: e.activation(out=tp[:, 0:T], in_=pb[:, 0:T], func=AF.Gelu), reads=[pk], writes=[tk])
                S.emit("pool", lambda e: e.tensor_tensor(out=tTt[:, m, 0:T], in0=tTt[:, m, 0:T], in1=tp[:, 0:T], op=ALU.mult),
                       reads=[tk, f"tT{m}"], writes=[f"tT{m}"])
            fm_block(c_u)

            for m in range(4):
                S.emit("dve", lambda e, m=m: e.tensor_tensor(out=cT[:, m, 0:T], in0=cT[:, m, 0:T], in1=mean_sb[:, 0:T], op=ALU.subtract),
                       reads=["mean_sb", f"cT{m}"], writes=[f"cT{m}"])
                S.emit("dve", lambda e, m=m: e.tensor_tensor(out=cT[:, m, 0:T], in0=cT[:, m, 0:T], in1=rstd_b[:, 0:T], op=ALU.mult),
                       reads=["rstd_b", f"cT{m}"], writes=[f"cT{m}"])
            for m in range(4):
                S.emit("act", lambda e, m=m: e.activation(out=cT[:, m, 0:T], in_=cT[:, m, 0:T], func=AF.Silu,
                                                          scale=bgc[l][:, m:m + 1], bias=bbc[l][:, m:m + 1]),
                       reads=[f"cT{m}", f"bgc{l}", f"bbc{l}"], writes=[f"cT{m}"])

            for m in range(4):
                pb, pk = next_bank()
                fns = []
                if sample:
                    for s, (c0, n) in enumerate(segs):
                        for hh in range(2):
                            h = 2 * m + hh
                            fns.append(lambda e, s=s, c0=c0, n=n, hh=hh, h=h, pb=pb: e.matmul(
                                pb[64 * hh:64 * hh + 64, c0:c0 + n], lhsT=vn[0:64, 0, 64 * h:64 * h + 64],
                                rhs=WmTs[l][0:64, s, h, 0:n], start=True, stop=True, tile_position=(0, 64 * hh)))
                    rk = ["vn0", f"WmTs{l}_0", f"WmTs{l}_1"]
                    lc = TS
                else:
                    for j in range(nj):
                        for hh in range(2):
                            h = 2 * m + hh
                            fns.append(lambda e, j=j, hh=hh, h=h, pb=pb: e.matmul(
                                pb[64 * hh:64 * hh + 64, 128 * j:128 * j + 128], lhsT=vn[:, j, 64 * h:64 * h + 64],
                                rhs=WmT[l][:, h, :], start=True, stop=True, tile_position=(0, 64 * hh)))
                    rk = [f"vn{j}" for j in range(nj)] + [f"WmT{l}"]
                    lc = 128
                S.emit("pe", fns, reads=rk, writes=[pk])
                nch = T // lc
                tp, tk = next_tmp()
                S.emit("dve", lambda e, m=m, pb=pb, tp=tp, nch=nch, lc=lc: e.scalar_tensor_tensor(
                    out=tp[:, 0:T].rearrange("p (c q) -> p c q", q=lc), in0=pb[:, 0:T].rearrange("p (c q) -> p c q", q=lc),
                    scalar=agc[l][:, m:m + 1], in1=Bt[l][:, m, 0:lc].unsqueeze(1).broadcast_to([128, nch, lc]),
                    op0=ALU.mult, op1=ALU.add), reads=[pk, f"agc{l}", f"Bt{l}_0", f"Bt{l}_1"], writes=[tk])
                if m < 3:
                    S.emit("pool", lambda e, m=m, tp=tp: e.tensor_tensor(out=yT[:, m, 0:T], in0=tp[:, 0:T], in1=tTt[:, m, 0:T], op=ALU.mult),
                           reads=[tk, f"tT{m}"], writes=[f"yT{m}"])
                else:
                    ya_last = (tp, tk)

            tp3, tk3 = ya_last
            S.emit("dve", lambda e, tp3=tp3: e.tensor_tensor(out=yT[:, 3, 0:T], in0=tp3[:, 0:T], in1=tTt[:, 3, 0:T], op=ALU.mult),
                   reads=[tk3, "tT3"], writes=["yT3"])
            for m in range(4):
                S.emit("dve", lambda e, m=m: e.tensor_tensor(out=yT[:, 4 + m, 0:T], in0=cT[:, m, 0:T], in1=sgb[:, m, 0:T], op=ALU.mult),
                       reads=[f"cT{m}", f"sgb{m}"], writes=[f"yT{4 + m}"])

            yTk = [f"yT{k}" for k in range(8)]
            wts = [w_block(), w_block(prefetch=False)]

            def wo_front(j):
                xr, xkey = xk[j]
                for n2 in range(2):
                    wt, wk = wts[n2]
                    pb, pk = next_bank()
                    S.emit("pe", [(lambda e, k=k, j=j, pb=pb, wt=wt: e.matmul(
                        pb[:PT, :], lhsT=yT[:, k, 128 * j:128 * j + PT], rhs=wt[:, k, :], start=(k == 0), stop=(k == 7)))
                        for k in range(8)], reads=[wk] + yTk, writes=[pk])
                    S.emit("dve", lambda e, xr=xr, pb=pb, n2=n2, j=j: e.scalar_tensor_tensor(
                        out=xr[:PT, 512 * n2:512 * n2 + 512], in0=xr[:PT, 512 * n2:512 * n2 + 512], scalar=ALPHA, in1=pb[:PT, :],
                        op0=ALU.mult, op1=ALU.add, accum_out=pst[:PT, j, 0, n2:n2 + 1]), reads=[pk, xkey], writes=[xkey, f"pst{j}_0{n2}"])
                    tp, tk = next_tmp()
                    S.emit("act", lambda e, xr=xr, tp=tp, n2=n2, j=j: e.activation(
                        out=tp[:PT, :], in_=xr[:PT, 512 * n2:512 * n2 + 512], func=AF.Square, accum_out=pst[:PT, j, 1, n2:n2 + 1]),
                        reads=[xkey], writes=[tk, f"pst{j}_1{n2}"])
                pk4 = [f"pst{j}_00", f"pst{j}_01", f"pst{j}_10", f"pst{j}_11"]
                S.emit("pool", lambda e, j=j: e.tensor_tensor(out=psc[:PT, j, 0:2], in0=pst[:PT, j, :, 0], in1=pst[:PT, j, :, 1], op=ALU.add),
                       reads=pk4, writes=[f"psc{j}"])
                S.emit("pool", lambda e, j=j: e.tensor_scalar(out=psc[:PT, j, 0:2], in0=psc[:PT, j, 0:2], scalar1=1.0 / D, scalar2=None, op0=ALU.mult),
                       reads=[f"psc{j}"], writes=[f"psc{j}"])
                S.emit("pool", lambda e, j=j: e.tensor_tensor(out=psc[:PT, j, 2:3], in0=psc[:PT, j, 0:1], in1=psc[:PT, j, 0:1], op=ALU.mult),
                       reads=[f"psc{j}"], writes=[f"psc{j}"])
                S.emit("pool", lambda e, j=j: e.tensor_scalar(out=psc[:PT, j, 3:4], in0=psc[:PT, j, 1:2], scalar1=LN_EPS, scalar2=None, op0=ALU.add),
                       reads=[f"psc{j}"], writes=[f"psc{j}"])
                S.emit("pool", lambda e, j=j: e.tensor_tensor(out=psc[:PT, j, 3:4], in0=psc[:PT, j, 3:4], in1=psc[:PT, j, 2:3], op=ALU.subtract),
                       reads=[f"psc{j}"], writes=[f"psc{j}"])
                rsqrt_pool(sm[1][:PT, j:j + 1], psc[:PT, j, 3:4], [f"psc{j}"], f"pl1_{j}", (PT, 1))

            def wo_back(j):
                xr, xkey = xk[j]
                S.emit("dve", lambda e, xr=xr, j=j: e.scalar_tensor_tensor(
                    out=xr[:PT, :], in0=xr[:PT, :], scalar=psc[:PT, j, 0:1], in1=pg[l][:PT, :], op0=ALU.subtract, op1=ALU.mult),
                    reads=[xkey, f"psc{j}", f"pg{l}"], writes=[xkey])
                S.emit("dve", lambda e, xr=xr, j=j: e.scalar_tensor_tensor(
                    out=xr[:PT, :], in0=xr[:PT, :], scalar=sm[1][:PT, j:j + 1], in1=pbb[l][:PT, :], op0=ALU.mult, op1=ALU.add),
                    reads=[xkey, f"pl1_{j}", f"pbb{l}"], writes=[xkey])
                if y_out is not None:
                    S.emit("sp", lambda e, xr=xr, j=j: e.dma_start(out=y_out[j], in_=xr[:PT, :]), reads=[xkey], dma="d_o" + xkey[-1])

            for j in range(nj):
                wo_front(j)
                if j >= 1:
                    wo_back(j - 1)
            wo_back(nj - 1)

        tiles_per_seq = seq_len // TT
        n_tiles = n_seq * tiles_per_seq
        n_tl = n_tiles + (0 if "nosample" in _DBG else 1)
        for t in range(n_tl):
            for l in range(DEPTH):
                plan.extend((l, b, t == 0) for b in range(8))

        def chunk_idx(t, j):
            return (4 * t + j) % 7

        def emit_xload(t, js):
            b = t // tiles_per_seq
            ti = t % tiles_per_seq
            for j in js:
                idx = chunk_idx(t, j)
                S.emit("sp", lambda e, idx=idx, b=b, ti=ti, j=j: e.dma_start(
                    out=xres[idx][:, :], in_=xp[b, ti * TT + 128 * j:ti * TT + 128 * j + 128, :]),
                    writes=[f"xres{idx}"], dma=f"d_x{idx}")

        emit_xload(0, [0, 1, 2, 3])
        sidx = chunk_idx(n_tiles, 0)
        for t in range(n_tiles):
            b = t // tiles_per_seq
            ti = t % tiles_per_seq
            xk = [(xres[chunk_idx(t, j)], f"xres{chunk_idx(t, j)}") for j in range(4)]
            if t + 1 < n_tiles:
                emit_xload(t + 1, [0, 1, 2])
            else:
                S.emit("sp", lambda e: e.dma_start(out=xres[sidx][0:n_samp * DEC_SEQ, :], in_=xs.rearrange("b t d -> (b t) d")),
                       writes=[f"xres{sidx}"], dma=f"d_x{sidx}")
                for l in range(DEPTH):
                    c = chunk_idx(n_tiles, 1 + l)
                    av_chunk[l] = c
                    S.emit("sp", [lambda e, c=c, l=l: e.dma_start(out=xres[c][:, 0:512], in_=a_ln_g[l].partition_broadcast(128)),
                                  lambda e, c=c, l=l: e.dma_start(out=xres[c][:, 512:1024], in_=a_ln_b[l].partition_broadcast(128))],
                           writes=[f"xres{c}"], dma=f"d_x{c}")
            for l in range(DEPTH):
                tile_layer(l, TT, [(0, TT)], xk, ti == 0, ti == tiles_per_seq - 1, False, None,
                           [ncp[l, b]],
                           [yp[b, ti * TT + 128 * j:ti * TT + 128 * j + 128, :] for j in range(4)] if l == DEPTH - 1 else None,
                           None)
            if t + 1 < n_tiles:
                emit_xload(t + 1, [3])

        xr0 = (xres[sidx], f"xres{sidx}")
        ssegs = [(DEC_SEQ * s, DEC_SEQ) for s in range(n_samp)]
        for l in range(DEPTH if "nosample" not in _DBG else 0):
            tile_layer(l, n_samp * DEC_SEQ, ssegs, [xr0], False, True, True, 0,
                       [ncs[l, s] for s in range(n_samp)],
                       [ys.rearrange("b t d -> (b t) d")] if l == DEPTH - 1 else None,
                       [nav[l, s] for s in range(n_samp)])

        S.wait_all("sp", ["d_o0", "d_o1", "d_o2", "d_o3", "d_o4", "d_o5", "d_o6", "d_x5", "d_x6", "d_oc", "d_ov", "d_ws0", "d_ws1", "d_ws2", "d_ws3", "d_m", "d_m2", "d_s", "d_x0", "d_x1", "d_x2", "d_x3", "d_x4", "d_x5", "d_x6", "d_o5", "d_o6", "d_c", "d_pm"])

        with nc.allow_non_contiguous_dma(reason="tiny parameter gathers in the prologue"):
            with nc.Block() as block:
                @block.tensor
                def _(e):
                    S.replay("pe", e)

                @block.scalar
                def _(e):
                    S.replay("act", e)

                @block.vector
                def _(e):
                    S.replay("dve", e)

                @block.gpsimd
                def _(e):
                    S.replay("pool", e)

                @block.sync
                def _(e):
                    S.replay("sp", e)
    return nc


_CONSTS = None


def _consts():
    global _CONSTS
    if _CONSTS is None:
        k = np.arange(128)
        _CONSTS = {
            "c_ident": np.eye(128, dtype=np.float32),
            "c_tril": (k[:, None] <= k[None, :]).astype(np.float32),
            "c_i32": (k[:, None] % 32 == np.arange(32)[None, :]).astype(np.float32),
        }
    return _CONSTS


_NC_CACHE = {}


def kernel(x_prompt, x_sample, cache_conv, w_in, a_ln_g, a_ln_b, a_ws, a_bias, b_conv_w, b_conv_b,
           b_ln_g, b_ln_b, w_out, post_ln_g, post_ln_b):
    f = lambda a: np.ascontiguousarray(np.asarray(a, dtype=np.float32))
    x_prompt, x_sample, cache_conv = f(x_prompt), f(x_sample), f(cache_conv)
    shared = {"w_in": f(w_in), "w_out": f(w_out), "a_ln_g": f(a_ln_g), "a_ln_b": f(a_ln_b), "a_ws": f(a_ws),
              "a_bias": f(a_bias), "b_conv_w": f(b_conv_w), "b_conv_b": f(b_conv_b), "b_ln_g": f(b_ln_g),
              "b_ln_b": f(b_ln_b), "post_ln_g": f(post_ln_g), "post_ln_b": f(post_ln_b)}
    shared.update(_consts())
    nb, seq_len = x_prompt.shape[0], x_prompt.shape[1]
    n_seq = nb // NCORES
    n_samp = x_sample.shape[0] // NCORES
    key = (n_seq, seq_len, n_samp)
    if key not in _NC_CACHE:
        _NC_CACHE[key] = build_nc(n_seq, seq_len, n_samp)
    nc = _NC_CACHE[key]
    in_maps = []
    for c in range(NCORES):
        m = dict(shared)
        m["xp"] = np.ascontiguousarray(x_prompt[c * n_seq:(c + 1) * n_seq])
        m["xs"] = np.ascontiguousarray(x_sample[c * n_samp:(c + 1) * n_samp])
        m["cc"] = np.ascontiguousarray(cache_conv[:, c * n_samp:(c + 1) * n_samp])
        in_maps.append(m)
    res = run_bass_kernel_spmd(nc, in_maps, core_ids=list(range(NCORES)))
    R = res.results
    y_p = np.concatenate([r["yp"] for r in R], axis=0)
    y_s = np.concatenate([r["ys"] for r in R], axis=0)
    ncp = np.concatenate([r["ncp"] for r in R], axis=1)
    ncs = np.concatenate([r["ncs"] for r in R], axis=1)
    nav = np.concatenate([r["nav"] for r in R], axis=1)
    return (y_p.astype(np.float32), y_s.astype(np.float32), ncp.astype(np.float32), ncs.astype(np.float32),
            nav.astype(np.float32))
```

```python
import contextlib
import numpy as np
import concourse.bass as bass
import concourse.mybir as mybir
from concourse.bass_utils import run_bass_kernel_spmd

F32 = mybir.dt.float32
BF16 = mybir.dt.bfloat16
I32 = mybir.dt.int32
AF = mybir.ActivationFunctionType
ALU = mybir.AluOpType

NCORES = 8
D = 1024
DEPTH = 2
SEQ = 2048
BATCH = 32
DEC_BATCH = 16
DEC_SEQ = 32
WA = 512
CONV_K = 31
HIST = 30
ALPHA = float((2 * DEPTH) ** 0.25)
LN_EPS = 1e-5
TT = 512
NSLOT = 4
BLOCKS = [("in", 2048), ("in", 1536), ("in", 2560), ("in", 512), ("in", 1024), ("in", 0),
          ("out", 0), ("out", 512)]
B_BG, B_BV, B_GB, B_V, B_GA, B_U, B_O0, B_O1 = range(8)


class _Eng:
    def __init__(self, name, sem):
        self.name = name
        self.sem = sem
        self.count = 0
        self.ops = []
        self.waited = {}


class Sched:
    def __init__(self):
        self.engs = {}
        self.last_w = {}
        self.readers = {}

    def add_engine(self, name, sem):
        self.engs[name] = _Eng(name, sem)

    def _deps(self, reads, writes):
        deps = {}

        def add(e, v):
            if v > deps.get(e, 0):
                deps[e] = v
        for b in reads:
            lw = self.last_w.get(b)
            if lw:
                add(*lw)
        for b in writes:
            lw = self.last_w.get(b)
            if lw:
                add(*lw)
            for e, v in self.readers.get(b, {}).items():
                add(e, v)
        return deps

    def emit(self, eng, fns, reads=(), writes=(), dma=None):
        E = self.engs[eng]
        if not isinstance(fns, (list, tuple)):
            fns = [fns]
        deps = self._deps(reads, writes)
        done = self.engs[dma] if dma else E
        for e, v in deps.items():
            if e == eng and eng == "pe":
                continue
            if v <= E.waited.get(e, 0):
                continue
            E.waited[e] = v
            E.ops.append(("wait", (e, v)))
        if dma:
            for fn in fns:
                done.count += 16
                E.ops.append(("dma", (fn, dma)))
        else:
            for fn in fns[:-1]:
                E.ops.append(("op", fn))
            done.count += 1
            E.ops.append(("opinc", fns[-1]))
        val = done.count
        dn = done.name
        for b in writes:
            self.last_w[b] = (dn, val)
            self.readers[b] = {}
        for b in reads:
            r = self.readers.setdefault(b, {})
            if val > r.get(dn, 0):
                r[dn] = val

    def settle(self, dma, keys):
        c = self.engs[dma].count
        for k in keys:
            self.last_w[k] = (dma, c)

    def wait_all(self, eng, targets):
        E = self.engs[eng]
        for t in targets:
            T = self.engs[t]
            if T.count > E.waited.get(t, 0):
                E.waited[t] = T.count
                E.ops.append(("wait", (t, T.count)))

    def replay(self, eng, handle):
        E = self.engs[eng]
        for kind, p in E.ops:
            if kind == "wait":
                e, v = p
                handle.wait_ge(self.engs[e].sem, v)
            elif kind == "op":
                p(handle)
            elif kind == "opinc":
                p(handle).then_inc(E.sem, 1)
            else:
                fn, d = p
                fn(handle).then_inc(self.engs[d].sem, 16)


import os
_DBG = set(os.environ.get("KDEBUG", "").split(","))


def build_nc(n_seq=4, seq_len=SEQ, n_samp=2):
    assert seq_len % TT == 0
    nc = bass.Bass("TRN2", target_bir_lowering=False)
    dr = lambda name, shape, dt=F32, kind="ExternalInput": nc.dram_tensor(name, shape, dt, kind=kind).ap()
    xp = dr("xp", [n_seq, seq_len, D])
    xs = dr("xs", [n_samp, DEC_SEQ, D])
    cc = dr("cc", [DEPTH, n_samp, HIST, WA])
    w_in = dr("w_in", [DEPTH, D, 3072])
    w_out = dr("w_out", [DEPTH, D, D])
    a_ln_g = dr("a_ln_g", [DEPTH, WA])
    a_ln_b = dr("a_ln_b", [DEPTH, WA])
    a_ws = dr("a_ws", [DEPTH, 8, 128, 128])
    a_bias = dr("a_bias", [DEPTH, 8, 128])
    b_conv_w = dr("b_conv_w", [DEPTH, CONV_K, WA])
    b_conv_b = dr("b_conv_b", [DEPTH, WA])
    b_ln_g = dr("b_ln_g", [DEPTH, WA])
    b_ln_b = dr("b_ln_b", [DEPTH, WA])
    post_ln_g = dr("post_ln_g", [DEPTH, D])
    post_ln_b = dr("post_ln_b", [DEPTH, D])
    c_ident = dr("c_ident", [128, 128])
    c_tril = dr("c_tril", [128, 128])
    c_i32 = dr("c_i32", [128, 32])
    yp = dr("yp", [n_seq, seq_len, D], kind="ExternalOutput")
    ys = dr("ys", [n_samp, DEC_SEQ, D], kind="ExternalOutput")
    ncp = dr("ncp", [DEPTH, n_seq, HIST, WA], kind="ExternalOutput")
    ncs = dr("ncs", [DEPTH, n_samp, HIST, WA], kind="ExternalOutput")
    nav = dr("nav", [DEPTH, n_samp, DEC_SEQ, WA], kind="ExternalOutput")
    wsc = dr("wsc", [DEPTH, 8, 128, 8 * 512], BF16, kind="Internal")

    es = contextlib.ExitStack()
    with es:
        sb = lambda name, shape, dt=F32: es.enter_context(nc.sbuf_tensor(name, shape, dt))
        S = Sched()
        for n in ["pe", "act", "dve", "pool", "sp", "d_c", "d_w0", "d_w1", "d_w2", "d_w3",
                  "d_s", "d_o0", "d_o1", "d_o2", "d_o3", "d_o4", "d_o5", "d_o6", "d_x5", "d_x6", "d_oc", "d_ov", "d_m", "d_pw0", "d_pw1", "d_pw2", "d_pw3", "d_pm", "d_ws0", "d_ws1", "d_ws2", "d_ws3", "d_x0", "d_x1", "d_x2", "d_x3", "d_x4", "d_m2"]:
            S.add_engine(n, es.enter_context(nc.semaphore("s_" + n)))

        xres = [sb(f"xres{i}", [128, D]) for i in range(7)]
        yT = sb("yT", [128, 8, TT], BF16)
        xbf = yT[:].rearrange("p k t -> p (k t)").rearrange("p (j d) -> p j d", d=D)
        xT = sb("xT", [128, 8, TT], BF16)
        tmpf = [sb(f"tmpf{i}", [128, TT]) for i in range(4)]
        tTt = sb("tT", [128, 4, TT])
        vn = sb("vn", [128, 4, WA], BF16)
        a2T = [sb(f"a2T{l}", [128, 4, 544], BF16) for l in range(DEPTH)]
        Sb = sb("Sb", [128, 16, 540], BF16)
        sgb = sb("sgb", [128, 4, TT])
        cT = sb("cT", [128, 4, TT])
        cbf = sb("cbf", [128, 4, TT], BF16)
        csq = sb("csq", [128, 4, TT], BF16)
        mean_sb = sb("mean_sb", [128, TT])
        rstd_b = sb("rstd_b", [128, TT])
        nt_a = sb("nt_a", [128, TT])
        alast = sb("alast", [128, 4, 2, 32])
        al_p = [sb(f"al_p{i}", [128, 4, 2, 32], BF16) for i in range(3)]
        al_r = sb("al_r", [128, 4, 2, 32])
        vf32 = sb("ostage", [64, WA])
        ostage = vf32
        mhalf = sb("mhalf", [128, 8])
        chbf = sb("chbf", [32, 2, WA], BF16)
        st6 = sb("st6", [128, 8, 6])
        st6v = sb("st6v", [128, 4, 6])
        one1 = sb("one1", [128, 2])
        junk1 = sb("junk1", [128, 2])
        pst = sb("pst", [128, 4, 2, 2])
        psc = sb("psc", [128, 4, 4])
        mv = sb("mv", [128, 4, 2])
        sm = [sb(f"sm{i}", [128, 4]) for i in range(5)]
        wr = [sb(f"wr{i}", [128, 8, 512], BF16) for i in range(NSLOT)]
        Wst = [sb(f"Wst{l}", [128, 16, 8, 32], BF16) for l in range(DEPTH)]
        wTt = sb("wTt", [128, 16, 8])
        WmT = [sb(f"WmT{l}", [128, 8, 128], BF16) for l in range(DEPTH)]
        WmTs = [sb(f"WmTs{l}", [64, 2, 8, 32], BF16) for l in range(DEPTH)]
        Bt = [sb(f"Bt{l}", [128, 4, 128]) for l in range(DEPTH)]
        pg = [sb(f"pg{l}", [128, D]) for l in range(DEPTH)]
        pbb = [sb(f"pbb{l}", [128, D]) for l in range(DEPTH)]
        av_chunk = [5, 6]
        agc = [sb(f"agc{l}", [128, 4]) for l in range(DEPTH)]
        cbc = [sb(f"cbc{l}", [128, 4]) for l in range(DEPTH)]
        bgc = [sb(f"bgc{l}", [128, 4]) for l in range(DEPTH)]
        bbc = [sb(f"bbc{l}", [128, 4]) for l in range(DEPTH)]
        ident_f = sb("ident_f", [128, 128])
        identb = sb("identb", [128, 128], BF16)
        tril_f = sb("tril_f", [128, 128])
        trilb = sb("trilb", [128, 128], BF16)
        i32m = sb("i32m", [128, 32])
        onesb = sb("onesb", [128, 128], BF16)
        wsq = cT[:].rearrange("p m t -> p (m t)")[:, 0:1024].rearrange("p (h k) -> p h k", k=128)
        wsqb = cbf[:].rearrange("p m t -> p (m t)")[:, 0:1024].rearrange("p (h k) -> p h k", k=128)
        vbb = csq[:].rearrange("p m t -> p (m t)")[:, 0:WA]
        abias_bc = sgb[:].rearrange("p m t -> p (m t)")[:, 0:1024].rearrange("p (h q) -> p h q", q=128)

        banks = [es.enter_context(nc.psum_tensor(f"pb{i}", [128, 512], F32)) for i in range(6)]
        ptrs = [es.enter_context(nc.psum_tensor(f"ptr{i}", [128, 1024], BF16)) for i in range(2)]

        class _Ptr:
            def __getitem__(self, idx):
                p, i, c = idx
                return ptrs[i][p, c]
        ptr = _Ptr()
        bank_i = [0]

        def next_bank():
            i = bank_i[0] % 6
            bank_i[0] += 1
            return banks[i], f"pb{i}"
        ptr_i = [0]

        def next_ptr():
            i = ptr_i[0] % 2
            ptr_i[0] += 1
            return i, f"ptr{i}"
        tmp_i = [0]

        def next_tmp():
            i = tmp_i[0] % 4
            tmp_i[0] += 1
            return tmpf[i], f"tmpf{i}"

        def rsqrt_pool(y, x, reads, ykey, shape):
            ex = mhalf[:shape[0], 0:shape[1]]
            S.emit("pool", lambda e: e.tensor_tensor(out=y, in0=x, in1=ex, op=ALU.pow), reads=list(reads) + ["mhalf"], writes=[ykey])

        def rsqrt_dve(y, x, ta, xh, reads, ykey, iters=3, takey=None, xhkey=None):
            keys = [ykey, takey or (ykey + "_ta"), xhkey or (ykey + "_xh")]
            S.emit("dve", lambda e: e.tensor_scalar(out=y.bitcast(I32), in0=x.bitcast(I32), scalar1=1, scalar2=None,
                                                    op0=ALU.arith_shift_right), reads=reads, writes=[keys[0]])
            S.emit("dve", lambda e: e.tensor_scalar(out=y.bitcast(I32), in0=y.bitcast(I32), scalar1=-1,
                                                    scalar2=0x5f3759df, op0=ALU.mult, op1=ALU.add),
                   reads=[keys[0]], writes=[keys[0]])
            S.emit("dve", lambda e: e.tensor_scalar(out=xh, in0=x, scalar1=-0.5, scalar2=None, op0=ALU.mult),
                   reads=reads, writes=[keys[2]])
            for _ in range(iters):
                S.emit("dve", lambda e: e.tensor_tensor(out=ta, in0=y, in1=y, op=ALU.mult), reads=[keys[0]], writes=[keys[1]])
                S.emit("dve", lambda e: e.tensor_tensor(out=ta, in0=ta, in1=xh, op=ALU.mult), reads=[keys[1], keys[2]],
                       writes=[keys[1]])
                S.emit("dve", lambda e: e.scalar_tensor_tensor(out=y, in0=ta, scalar=1.5, in1=y, op0=ALU.add, op1=ALU.mult),
                       reads=[keys[0], keys[1]], writes=[keys[0]])

        CT4 = [f"cT{m}" for m in range(4)]
        SGB4 = [f"sgb{m}" for m in range(4)]
        CBF4 = [f"cbf{m}" for m in range(4)]
        CSQ4 = [f"csq{m}" for m in range(4)]
        pro = []

        def pload(dst_ap, src_ap, key, eng="sp"):
            S.emit(eng, lambda e: e.dma_start(out=dst_ap, in_=src_ap), writes=[key], dma="d_c")
            pro.append(key)
        pload(ident_f[:], c_ident, "ident_f")
        pload(tril_f[:], c_tril, "tril_f")
        pload(i32m[:], c_i32, "i32m")
        for l in range(DEPTH):
            pload(pg[l][:], post_ln_g[l].partition_broadcast(128), f"pg{l}")
            pload(pbb[l][:], post_ln_b[l].partition_broadcast(128), f"pbb{l}")
            pload(xres[av_chunk[l]][:, 512:1024], a_ln_b[l].partition_broadcast(128), f"xres{av_chunk[l]}")
            pload(agc[l][:], a_ln_g[l].rearrange("(m p) -> p m", p=128), f"agc{l}")
            pload(cbc[l][:], b_conv_b[l].rearrange("(m p) -> p m", p=128), f"cbc{l}")
            pload(bgc[l][:], b_ln_g[l].rearrange("(m p) -> p m", p=128), f"bgc{l}")
            pload(bbc[l][:], b_ln_b[l].rearrange("(m p) -> p m", p=128), f"bbc{l}")
        S.settle("d_c", pro)
        S.emit("dve", lambda e: e.tensor_copy(out=identb[:], in_=ident_f[:]), reads=["ident_f"], writes=["identb"])
        S.emit("dve", lambda e: e.tensor_copy(out=trilb[:], in_=tril_f[:]), reads=["tril_f"], writes=["trilb"])
        S.emit("pool", lambda e: e.memset(onesb[:], 1.0 / 512.0), writes=["onesb"])
        S.emit("pool", lambda e: e.memset(mhalf[:], -0.5), writes=["mhalf"])
        S.emit("pool", lambda e: e.memset(one1[:], 1.0), writes=["one1"])
        for l in range(DEPTH):
            S.emit("pool", lambda e, l=l: e.memset(a2T[l][:], 0.0), writes=[f"a2T{l}"])
        S.emit("pool", lambda e: e.memset(Sb[:], 0.0), writes=["Sb"])
        S.emit("pool", lambda e: e.memset(chbf[:], 0.0), writes=["chbf0", "chbf1"])

        for l in range(DEPTH):
            S.emit("sp", lambda e, l=l: e.dma_start(out=wsq, in_=a_ws[l].rearrange("h q k -> q h k")), writes=CT4, dma="d_m")
            S.emit("sp", lambda e, l=l: e.dma_start(out=abias_bc, in_=a_bias[l].rearrange("h q -> (h q)").partition_broadcast(128)
                                                    .rearrange("p (h q) -> p h q", q=128)), writes=SGB4, dma="d_m")
            S.emit("pool", lambda e: e.memset(wTt[:], 0.0), writes=["wTt"])
            cw = []
            for s in range(4):
                for mm in range(8):
                    j = 4 * mm + s
                    if j >= CONV_K:
                        continue
                    cw.append(lambda e, l=l, s=s, mm=mm, j=j: e.dma_start(
                        out=wTt[32 * s:32 * s + 32, :, mm], in_=b_conv_w[l, j, :].rearrange("(g c) -> c g", c=32)))
            S.emit("sp", cw, writes=["wTt"], dma="d_m")
            S.settle("d_m", CT4 + SGB4 + ["wTt"])
            S.emit("dve", lambda e: e.tensor_copy(out=wsqb, in_=wsq), reads=CT4, writes=CBF4)
            for hb in range(2):
                pi, pk = next_ptr()
                S.emit("pe", [(lambda e, h=h, pi=pi: e.transpose(out=ptr[:, pi, 128 * (h % 4):128 * (h % 4) + 128],
                                                                 in_=wsqb[:, h, :], identity=identb[:]))
                              for h in range(4 * hb, 4 * hb + 4)], reads=CBF4 + ["identb"], writes=[pk])
                S.emit("dve", lambda e, l=l, hb=hb, pi=pi: e.tensor_tensor(
                    out=WmT[l][:, 4 * hb:4 * hb + 4, :], in0=ptr[:, pi, 0:512].rearrange("p (h q) -> p h q", q=128),
                    in1=trilb[:].unsqueeze(1).broadcast_to([128, 4, 128]), op=ALU.mult),
                    reads=[pk, "trilb"], writes=[f"WmT{l}"])
            S.emit("pool", lambda e, l=l: e.memset(WmTs[l][:], 0.0), writes=[f"WmTs{l}_0", f"WmTs{l}_1"])
            S.emit("sp", [(lambda e, l=l, s2=s2: e.dma_start(out=WmTs[l][32 * s2:32 * s2 + 32, s2, :, :], in_=WmT[l][0:32, :, 0:32]))
                          for s2 in range(2)], reads=[f"WmT{l}"], writes=[f"WmTs{l}_0", f"WmTs{l}_1"], dma="d_m2")
            S.emit("dve", lambda e, c=av_chunk[l]: e.tensor_copy(out=vbb, in_=xres[c][:, 512:1024]), reads=[f"xres{av_chunk[l]}"], writes=CSQ4)
            pbk, pkk = next_bank()
            S.emit("pe", [(lambda e, l=l, m=m, hh=hh, pbk=pbk: e.matmul(
                pbk[64 * hh:64 * hh + 64, 128 * m:128 * m + 128], lhsT=vbb[:, 64 * (2 * m + hh):64 * (2 * m + hh) + 64],
                rhs=WmT[l][:, 2 * m + hh, :], start=True, stop=True, tile_position=(0, 64 * hh)))
                for m in range(4) for hh in range(2)], reads=CSQ4 + [f"WmT{l}"], writes=[pkk])
            ab4 = abias_bc.rearrange("p (m two) q -> p m two q", two=2)
            for hh in range(2):
                S.emit("dve", lambda e, l=l, hh=hh, pbk=pbk: e.tensor_tensor(
                    out=Bt[l][64 * hh:64 * hh + 64, :, :],
                    in0=pbk[64 * hh:64 * hh + 64, :].rearrange("p (m q) -> p m q", q=128),
                    in1=ab4[64 * hh:64 * hh + 64, :, hh, :], op=ALU.add),
                    reads=[pkk] + SGB4, writes=[f"Bt{l}_{hh}"])
            S.emit("dve", lambda e, l=l: e.scalar_tensor_tensor(
                out=Wst[l][:].rearrange("p g m c -> p (g m) c"),
                in0=wTt[:].rearrange("p g m -> p (g m)").unsqueeze(2).broadcast_to([128, 128, 32]), scalar=0.5,
                in1=i32m[:].unsqueeze(1).broadcast_to([128, 128, 32]), op0=ALU.mult, op1=ALU.mult),
                reads=["wTt", "i32m"], writes=[f"Wst{l}"])

        S.settle("d_m2", [f"WmTs{l}_{s}" for l in range(DEPTH) for s in range(2)])

        gblk = [0]
        wnext = [0]
        plan = []

        def w_src(l, b):
            kind, c0 = BLOCKS[b]
            src = (w_in if kind == "in" else w_out)[l]
            return src[:, c0:c0 + 512].rearrange("(k p) e -> p k e", p=128)

        def w_prefetch(upto):
            while wnext[0] <= upto and wnext[0] < len(plan):
                g = wnext[0]
                l, b, first = plan[g]
                slot = g % NSLOT
                if first:
                    S.emit("pool", lambda e, l=l, b=b, slot=slot: e.dma_start(out=wr[slot][:], in_=w_src(l, b)),
                           writes=[f"wr{slot}"], dma=f"d_pw{slot}")
                    S.emit("sp", lambda e, l=l, b=b, slot=slot: e.dma_start(
                        out=wsc[l, b], in_=wr[slot][:].rearrange("p k e -> p (k e)")),
                        reads=[f"wr{slot}"], writes=[f"wsc{l}_{b}"], dma=f"d_ws{slot}")
                else:
                    S.emit("sp", lambda e, l=l, b=b, slot=slot: e.dma_start(
                        out=wr[slot][:].rearrange("p k e -> p (k e)"), in_=wsc[l, b]),
                        reads=[f"wsc{l}_{b}"], writes=[f"wr{slot}"], dma=f"d_w{slot}")
                wnext[0] += 1

        def w_block(prefetch=True):
            g = gblk[0]
            gblk[0] += 1
            if prefetch:
                w_prefetch(g + NSLOT - 1)
            slot = g % NSLOT
            return wr[slot], f"wr{slot}"

        def tile_layer(l, T, segs, xk, first_in_seq, last_in_seq, sample, hist_src, conv_out, y_out, av_out):
            nj = len(xk)
            PT = min(T, 128)
            nseg = len(segs)
            TS = segs[0][1]
            if sample:
                a2v = a2T[l][:, :, 0:128].rearrange("p m (s c) -> p m s c", c=64)
                Sv = Sb[:, :, 0:120].rearrange("p g (s c) -> p g s c", c=60)
            else:
                a2v = a2T[l][:, :, :].unsqueeze(2)
                Sv = Sb[:, :, :].unsqueeze(2)
            ka2 = f"a2T{l}"
            for j in range(nj):
                S.emit("act", lambda e, j=j: e.copy(out=xbf[:PT, j, :], in_=xk[j][0][:PT, :]),
                       reads=[xk[j][1]], writes=[f"xbf{j}", f"yT{2 * j}", f"yT{2 * j + 1}"])
                pi, pk = next_ptr()
                S.emit("pe", [(lambda e, j=j, k=k, pi=pi: e.transpose(
                    out=ptr[:, pi, 128 * k:128 * k + PT], in_=xbf[:PT, j, 128 * k:128 * k + 128], identity=identb[:PT, :PT]))
                    for k in range(8)], reads=[f"xbf{j}", "identb"], writes=[pk])
                eng = "act" if j % 2 == 1 else "dve"
                src_v = lambda pi=pi: ptr[:, pi, 0:1024].rearrange("p (k t) -> p k t", t=128)[:, :, 0:PT]
                if eng == "act":
                    S.emit("act", lambda e, j=j, src_v=src_v: e.copy(out=xT[:, :, 128 * j:128 * j + PT], in_=src_v()),
                           reads=[pk], writes=[f"xTj{j}"])
                else:
                    S.emit("dve", lambda e, j=j, src_v=src_v: e.tensor_copy(out=xT[:, :, 128 * j:128 * j + PT], in_=src_v()),
                           reads=[pk], writes=[f"xTj{j}"])
            xTk = [f"xTj{j}" for j in range(nj)]

            def fm_block(consume):
                wt, wk = w_block()
                for m in range(4):
                    pb, pk = next_bank()
                    S.emit("pe", [(lambda e, k=k, m=m, pb=pb, wt=wt: e.matmul(
                        pb[:, 0:T], lhsT=wt[:, k, 128 * m:128 * m + 128], rhs=xT[:, k, 0:T], start=(k == 0), stop=(k == 7)))
                        for k in range(8)], reads=[wk] + xTk, writes=[pk])
                    consume(m, pb, pk)

            def fm_block_split(consume):
                wt, wk = w_block()
                jobs = [next_bank() for _ in range(4)]
                for half in range(2):
                    c0 = 256 * half
                    for m in range(4):
                        pb, pk = jobs[m]
                        S.emit("pe", [(lambda e, k=k, m=m, pb=pb, wt=wt, c0=c0: e.matmul(
                            pb[:, c0:c0 + 256], lhsT=wt[:, k, 128 * m:128 * m + 128], rhs=xT[:, k, c0:c0 + 256],
                            start=(k == 0), stop=(k == 7))) for k in range(8)],
                            reads=[wk, f"xTj{2 * half}", f"xTj{2 * half + 1}"], writes=[pk])
                for m in range(4):
                    consume(m, jobs[m][0], jobs[m][1])

            if hist_src is not None:
                for s in range(nseg):
                    S.emit("pool", lambda e, s=s: e.dma_start(out=chbf[0:HIST, s, :], in_=cc[l, hist_src + s]),
                           writes=[f"chbf{s}"], dma="d_pm")
                S.settle("d_pm", [f"chbf{s}" for s in range(nseg)])
                for s in range(nseg):
                    pi, pk = next_ptr()
                    S.emit("pe", [(lambda e, m=m, s=s, pi=pi: e.transpose(
                        out=ptr[:, pi, 32 * m:32 * m + 32], in_=chbf[0:32, s, 128 * m:128 * m + 128],
                        identity=identb[:32, :32])) for m in range(4)], reads=[f"chbf{s}", "identb"], writes=[pk])
                    S.emit("act", lambda e, s=s, pi=pi: e.mul(
                        out=a2v[:, :, s, 0:HIST], in_=ptr[:, pi, 0:128].rearrange("p (m c) -> p m c", c=32)[:, :, 0:HIST], mul=2.0),
                        reads=[pk], writes=[ka2])
            elif first_in_seq:
                S.emit("pool", lambda e: e.memset(a2T[l][:, :, 0:HIST], 0.0), writes=[ka2])
            else:
                S.emit("pool", lambda e: e.tensor_copy(out=a2T[l][:, :, 0:HIST], in_=a2T[l][:, :, TT:TT + HIST]),
                       reads=[ka2], writes=[ka2])

            ths = []

            def c_bg(m, pb, pk):
                tp, tk = next_tmp()
                ths.append((tp, tk))
                S.emit("act", lambda e: e.activation(out=tp[:, 0:T], in_=pb[:, 0:T], func=AF.Tanh, scale=0.5),
                       reads=[pk], writes=[tk])
            (fm_block_split if nj == 4 else fm_block)(c_bg)

            def c_bv(m, pb, pk):
                tp, tk = ths[m]
                fns = [(lambda e, s=s, c0=c0, n=n: e.scalar_tensor_tensor(
                    out=a2v[:, m, s, HIST:HIST + n], in0=tp[:, c0:c0 + n], scalar=1.0, in1=pb[:, c0:c0 + n],
                    op0=ALU.add, op1=ALU.mult)) for s, (c0, n) in enumerate(segs)]
                wr_keys = [ka2]
                if last_in_seq:
                    fns += [(lambda e, s=s, c0=c0, n=n: e.scalar_tensor_tensor(
                        out=alast[:, m, s, :], in0=tp[:, c0 + n - 32:c0 + n], scalar=1.0, in1=pb[:, c0 + n - 32:c0 + n],
                        op0=ALU.add, op1=ALU.mult)) for s, (c0, n) in enumerate(segs)]
                    wr_keys.append("alast")
                S.emit("dve", fns, reads=[tk, pk], writes=wr_keys)
            fm_block(c_bv)

            ncol = TS + 28
            S.emit("sp", [(lambda e, s=s, g=g, sg=sg: e.dma_start(
                out=Sv[32 * s:32 * s + 32, :, sg, 0:ncol].rearrange("p (m f) c -> p m f c", f=4)[:, :, g, :],
                in_=a2v[32 * g:32 * g + 32, :, sg, s:s + ncol])) for s in range(4) for g in range(4) for sg in range(nseg)],
                reads=[ka2], writes=["Sb"], dma="d_s")

            if last_in_seq and "noconvout" not in _DBG:
                S.emit("dve", lambda e: e.tensor_copy(out=al_p[0][:, :, 0:nseg, :], in_=alast[:, :, 0:nseg, :]), reads=["alast"], writes=["al_p0"])
                S.emit("dve", lambda e: e.tensor_tensor(out=al_r[:, :, 0:nseg, :], in0=alast[:, :, 0:nseg, :], in1=al_p[0][:, :, 0:nseg, :],
                                                        op=ALU.subtract), reads=["alast", "al_p0"], writes=["al_r"])
                S.emit("dve", lambda e: e.tensor_copy(out=al_p[1][:, :, 0:nseg, :], in_=al_r[:, :, 0:nseg, :]), reads=["al_r"], writes=["al_p1"])
                S.emit("dve", lambda e: e.tensor_tensor(out=al_r[:, :, 0:nseg, :], in0=al_r[:, :, 0:nseg, :], in1=al_p[1][:, :, 0:nseg, :],
                                                        op=ALU.subtract), reads=["al_r", "al_p1"], writes=["al_r"])
                S.emit("dve", lambda e: e.tensor_copy(out=al_p[2][:, :, 0:nseg, :], in_=al_r[:, :, 0:nseg, :]), reads=["al_r"], writes=["al_p2"])
                for s in range(nseg):
                    pb, pk = next_bank()
                    S.emit("pe", [(lambda e, m=m, i=i, s=s, pb=pb: e.matmul(
                        pb[0:32, 128 * m:128 * m + 128], lhsT=al_p[i][:, m, s, :], rhs=identb[:], start=(i == 0), stop=(i == 2)))
                        for m in range(4) for i in range(3)], reads=["al_p0", "al_p1", "al_p2", "identb"], writes=[pk])
                    S.emit("act", lambda e, pb=pb: e.mul(out=ostage[0:32, :], in_=pb[0:32, :], mul=0.5), reads=[pk], writes=["ostage"])
                    S.emit("sp", lambda e, s=s: e.dma_start(out=conv_out[s], in_=ostage[2:32, :]), reads=["ostage"], dma="d_oc")

            def c_gb(m, pb, pk):
                S.emit("act", lambda e: e.activation(out=sgb[:, m, 0:T], in_=pb[:, 0:T], func=AF.Silu), reads=[pk], writes=[f"sgb{m}"])
            fm_block(c_gb)

            wt, wk = w_block()
            gvs = []
            for j in range(nj):
                pb, pk = next_bank()
                S.emit("pe", [(lambda e, k=k, j=j, pb=pb, wt=wt: e.matmul(
                    pb[:PT, :], lhsT=xT[:, k, 128 * j:128 * j + PT], rhs=wt[:, k, :], start=(k == 0), stop=(k == 7)))
                    for k in range(8)], reads=[wk] + xTk, writes=[pk])
                tp, tk = next_tmp()
                gvs.append((tp, tk))
                S.emit("act", lambda e, pb=pb, tp=tp: e.activation(out=tp[:PT, :], in_=pb[:PT, :], func=AF.Gelu), reads=[pk], writes=[tk])
                S.emit("dve", lambda e, j=j, tp=tp: e.bn_stats(out=st6v[:PT, j, :], in_=tp[:PT, :]), reads=[tk], writes=[f"st6v_{j}"])
                S.emit("dve", lambda e, j=j: e.bn_aggr(out=mv[:PT, j, :], in_=st6v[:PT, j, :]), reads=[f"st6v_{j}"], writes=[f"mv{j}"])
            mvk = [f"mv{j}" for j in range(nj)]
            S.emit("pool", lambda e: e.tensor_scalar(out=sm[0][:PT, 0:nj], in0=mv[:PT, 0:nj, 1], scalar1=LN_EPS, scalar2=None, op0=ALU.add),
                   reads=mvk, writes=["sm0"])
            rsqrt_pool(sm[1][:PT, 0:nj], sm[0][:PT, 0:nj], ["sm0"], "sm1", (PT, nj))
            S.emit("dve", lambda e: e.scalar_tensor_tensor(out=sm[4][:PT, 0:nj], in0=mv[:PT, 0:nj, 0], scalar=-1.0, in1=sm[1][:PT, 0:nj],
                                                           op0=ALU.mult, op1=ALU.mult), reads=mvk + ["sm1"], writes=["sm4"])
            for j in range(nj):
                tp, tk = gvs[j]
                S.emit("act", lambda e, j=j, tp=tp: e.activation(out=vn[:PT, j, :], in_=tp[:PT, :], func=AF.Identity,
                                                                 scale=sm[1][:PT, j:j + 1], bias=sm[4][:PT, j:j + 1]),
                       reads=[tk, "sm1", "sm4"], writes=[f"vn{j}"])
                if sample:
                    S.emit("act", lambda e, j=j, tp=tp: e.activation(out=vf32[:PT, :], in_=tp[:PT, :], func=AF.Identity,
                                                                     scale=sm[1][:PT, j:j + 1], bias=sm[4][:PT, j:j + 1]),
                           reads=[tk, "sm1", "sm4"], writes=["ostage"])
                    avc = av_chunk[l]
                    S.emit("dve", lambda e, avc=avc: e.tensor_tensor(out=vf32[:PT, :], in0=vf32[:PT, :], in1=xres[avc][:PT, 0:512], op=ALU.mult),
                           reads=["ostage", f"xres{avc}"], writes=["ostage"])
                    S.emit("dve", lambda e, avc=avc: e.tensor_tensor(out=vf32[:PT, :], in0=vf32[:PT, :], in1=xres[avc][:PT, 512:1024], op=ALU.add),
                           reads=["ostage", f"xres{avc}"], writes=["ostage"])
                    S.emit("sp", [(lambda e, s=s, c0=c0, n=n: e.dma_start(out=av_out[s], in_=vf32[c0:c0 + n, :]))
                                  for s, (c0, n) in enumerate(segs)], reads=["ostage"], dma="d_ov")

            pbm, pkm = next_bank()
            pbq, pkq = next_bank()

            def stat_mm(m):
                S.emit("pe", lambda e, m=m: e.matmul(pbm[:, 0:T], lhsT=onesb[:], rhs=cbf[:, m, 0:T], start=(m == 0), stop=(m == 3)),
                       reads=["onesb", f"cbf{m}"], writes=[pkm])
                S.emit("pe", lambda e, m=m: e.matmul(pbq[:, 0:T], lhsT=onesb[:], rhs=csq[:, m, 0:T], start=(m == 0), stop=(m == 3)),
                       reads=["onesb", f"csq{m}"], writes=[pkq])

            for m in range(4):
                pb, pk = next_bank()
                fns = []
                for s, (c0, n) in enumerate(segs):
                    for mm in range(8):
                        for g in range(4):
                            fns.append(lambda e, m=m, s=s, c0=c0, n=n, mm=mm, g=g, pb=pb: e.matmul(
                                pb[32 * g:32 * g + 32, c0:c0 + n], lhsT=Wst[l][:, 4 * m + g, mm, :],
                                rhs=Sv[:, 4 * m + g, s, 4 * mm:4 * mm + n], start=(mm == 0), stop=(mm == 7),
                                tile_position=(0, 32 * g)))
                S.emit("pe", fns, reads=["Sb", f"Wst{l}"], writes=[pk])
                S.emit("act", lambda e, m=m, pb=pb: e.activation(out=cbf[:, m, 0:T], in_=pb[:, 0:T], func=AF.Identity,
                                                                 bias=cbc[l][:, m:m + 1], scale=1.0),
                       reads=[pk, f"cbc{l}"], writes=[f"cbf{m}"])
                S.emit("act", lambda e, m=m, pb=pb: e.activation(out=csq[:, m, 0:T], in_=pb[:, 0:T], func=AF.Square,
                                                                 bias=cbc[l][:, m:m + 1], scale=1.0),
                       reads=[pk, f"cbc{l}"], writes=[f"csq{m}"])
                S.emit("act", lambda e, m=m, pb=pb: e.activation(out=cT[:, m, 0:T], in_=pb[:, 0:T], func=AF.Identity,
                                                                 bias=cbc[l][:, m:m + 1], scale=1.0),
                       reads=[pk, f"cbc{l}"], writes=[f"cT{m}"])
                if m >= 1:
                    stat_mm(m - 1)
            stat_mm(3)

            S.emit("act", lambda e: e.copy(out=mean_sb[:, 0:T], in_=pbm[:, 0:T]), reads=[pkm], writes=["mean_sb"])
            S.emit("act", lambda e: e.activation(out=nt_a[:, 0:T], in_=pbm[:, 0:T], func=AF.Square), reads=[pkm], writes=["nt_a"])
            S.emit("act", lambda e: e.activation(out=junk1[:, 0:1], in_=one1[:, 0:1], func=AF.Ln), reads=["one1"], writes=["junk1"])
            S.emit("dve", lambda e: e.scalar_tensor_tensor(out=nt_a[:, 0:T], in0=pbq[:, 0:T], scalar=LN_EPS, in1=nt_a[:, 0:T],
                                                           op0=ALU.add, op1=ALU.subtract), reads=[pkq, "nt_a"], writes=["nt_a"])
            tpa, tka = next_tmp()
            tpb, tkb = next_tmp()
            S.emit("act", lambda e: e.activation(out=tpa[:, 0:T], in_=nt_a[:, 0:T], func=AF.Ln), reads=["nt_a"], writes=[tka])
            S.emit("act", lambda e: e.activation(out=rstd_b[:, 0:T], in_=tpa[:, 0:T], func=AF.Exp, scale=-0.5), reads=[tka], writes=["rstd_b"])
            S.emit("act", lambda e: e.activation(out=tpb[:, 0:T], in_=nt_a[:, 0:T], func=AF.Identity, scale=-0.5), reads=["nt_a"], writes=[tkb])
            S.emit("dve", lambda e: e.tensor_tensor(out=tpa[:, 0:T], in0=rstd_b[:, 0:T], in1=rstd_b[:, 0:T], op=ALU.mult), reads=["rstd_b"], writes=[tka])
            S.emit("dve", lambda e: e.tensor_tensor(out=tpa[:, 0:T], in0=tpa[:, 0:T], in1=tpb[:, 0:T], op=ALU.mult), reads=[tka, tkb], writes=[tka])
            S.emit("dve", lambda e: e.scalar_tensor_tensor(out=rstd_b[:, 0:T], in0=tpa[:, 0:T], scalar=1.5, in1=rstd_b[:, 0:T],
                                                           op0=ALU.add, op1=ALU.mult), reads=[tka, "rstd_b"], writes=["rstd_b"])
            def c_ga(m, pb, pk):
                S.emit("act", lambda e: e.activation(out=tTt[:, m, 0:T], in_=pb[:, 0:T], func=AF.Silu), reads=[pk], writes=[f"tT{m}"])
            fm_block(c_ga)

            def c_u(m, pb, pk):
                tp, tk = next_tmp()
                S.emit("act", lambda e: e.activation(out=tp[:, 0:T], in_=pb[:, 0:T], func=AF.Gelu), reads=[pk], writes=[tk])
                S.emit("pool", lambda e: e.tensor_tensor(out=tTt[:, m, 0:T], in0=tTt[:, m, 0:T], in1=tp[:, 0:T], op=ALU.mult),
                       reads=[tk, f"tT{m}"], writes=[f"tT{m}"])
            fm_block(c_u)

            for m in range(4):
                S.emit("dve", lambda e, m=m: e.tensor_tensor(out=cT[:, m, 0:T], in0=cT[:, m, 0:T], in1=mean_sb[:, 0:T], op=ALU.subtract),
                       reads=["mean_sb", f"cT{m}"], writes=[f"cT{m}"])
                S.emit("dve", lambda e, m=m: e.tensor_tensor(out=cT[:, m, 0:T], in0=cT[:, m, 0:T], in1=rstd_b[:, 0:T], op=ALU.mult),
                       reads=["rstd_b", f"cT{m}"], writes=[f"cT{m}"])
            for m in range(4):
                S.emit("act", lambda e, m=m: e.activation(out=cT[:, m, 0:T], in_=cT[:, m, 0:T], func=AF.Silu,
                                                          scale=bgc[l][:, m:m + 1], bias=bbc[l][:, m:m + 1]),
                       reads=[f"cT{m}", f"bgc{l}", f"bbc{l}"], writes=[f"cT{m}"])

            for m in range(4):
                pb, pk = next_bank()
                fns = []
                if sample:
                    for s, (c0, n) in enumerate(segs):
                        for hh in range(2):
                            h = 2 * m + hh
                            fns.append(lambda e, s=s, c0=c0, n=n, hh=hh, h=h, pb=pb: e.matmul(
                                pb[64 * hh:64 * hh + 64, c0:c0 + n], lhsT=vn[0:64, 0, 64 * h:64 * h + 64],
                                rhs=WmTs[l][0:64, s, h, 0:n], start=True, stop=True, tile_position=(0, 64 * hh)))
                    rk = ["vn0", f"WmTs{l}_0", f"WmTs{l}_1"]
                    lc = TS
                else:
                    for j in range(nj):
                        for hh in range(2):
                            h = 2 * m + hh
                            fns.append(lambda e, j=j, hh=hh, h=h, pb=pb: e.matmul(
                                pb[64 * hh:64 * hh + 64, 128 * j:128 * j + 128], lhsT=vn[:, j, 64 * h:64 * h + 64],
                                rhs=WmT[l][:, h, :], start=True, stop=True, tile_position=(0, 64 * hh)))
                    rk = [f"vn{j}" for j in range(nj)] + [f"WmT{l}"]
                    lc = 128
                S.emit("pe", fns, reads=rk, writes=[pk])
                nch = T // lc
                tp, tk = next_tmp()
                S.emit("dve", lambda e, m=m, pb=pb, tp=tp, nch=nch, lc=lc: e.scalar_tensor_tensor(
                    out=tp[:, 0:T].rearrange("p (c q) -> p c q", q=lc), in0=pb[:, 0:T].rearrange("p (c q) -> p c q", q=lc),
                    scalar=agc[l][:, m:m + 1], in1=Bt[l][:, m, 0:lc].unsqueeze(1).broadcast_to([128, nch, lc]),
                    op0=ALU.mult, op1=ALU.add), reads=[pk, f"agc{l}", f"Bt{l}_0", f"Bt{l}_1"], writes=[tk])
                if m < 3:
                    S.emit("pool", lambda e, m=m, tp=tp: e.tensor_tensor(out=yT[:, m, 0:T], in0=tp[:, 0:T], in1=tTt[:, m, 0:T], op=ALU.mult),
                           reads=[tk, f"tT{m}"], writes=[f"yT{m}"])
                else:
                    ya_last = (tp, tk)

            tp3, tk3 = ya_last
            S.emit("dve", lambda e, tp3=tp3: e.tensor_tensor(out=yT[:, 3, 0:T], in0=tp3[:, 0:T], in1=tTt[:, 3, 0:T], op=ALU.mult),
                   reads=[tk3, "tT3"], writes=["yT3"])
            for m in range(4):
                S.emit("dve", lambda e, m=m: e.tensor_tensor(out=yT[:, 4 + m, 0:T], in0=cT[:, m, 0:T], in1=sgb[:, m, 0:T], op=ALU.mult),
                       reads=[f"cT{m}", f"sgb{m}"], writes=[f"yT{4 + m}"])

            yTk = [f"yT{k}" for k in range(8)]
            wts = [w_block(), w_block(prefetch=False)]

            def wo_front(j):
                xr, xkey = xk[j]
                for n2 in range(2):
                    wt, wk = wts[n2]
                    pb, pk = next_bank()
                    S.emit("pe", [(lambda e, k=k, j=j, pb=pb, wt=wt: e.matmul(
                        pb[:PT, :], lhsT=yT[:, k, 128 * j:128 * j + PT], rhs=wt[:, k, :], start=(k == 0), stop=(k == 7)))
                        for k in range(8)], reads=[wk] + yTk, writes=[pk])
                    S.emit("dve", lambda e, xr=xr, pb=pb, n2=n2, j=j: e.scalar_tensor_tensor(
                        out=xr[:PT, 512 * n2:512 * n2 + 512], in0=xr[:PT, 512 * n2:512 * n2 + 512], scalar=ALPHA, in1=pb[:PT, :],
                        op0=ALU.mult, op1=ALU.add, accum_out=pst[:PT, j, 0, n2:n2 + 1]), reads=[pk, xkey], writes=[xkey, f"pst{j}_0{n2}"])
                    tp, tk = next_tmp()
                    S.emit("act", lambda e, xr=xr, tp=tp, n2=n2, j=j: e.activation(
                        out=tp[:PT, :], in_=xr[:PT, 512 * n2:512 * n2 + 512], func=AF.Square, accum_out=pst[:PT, j, 1, n2:n2 + 1]),
                        reads=[xkey], writes=[tk, f"pst{j}_1{n2}"])
                pk4 = [f"pst{j}_00", f"pst{j}_01", f"pst{j}_10", f"pst{j}_11"]
                S.emit("pool", lambda e, j=j: e.tensor_tensor(out=psc[:PT, j, 0:2], in0=pst[:PT, j, :, 0], in1=pst[:PT, j, :, 1], op=ALU.add),
                       reads=pk4, writes=[f"psc{j}"])
                S.emit("pool", lambda e, j=j: e.tensor_scalar(out=psc[:PT, j, 0:2], in0=psc[:PT, j, 0:2], scalar1=1.0 / D, scalar2=None, op0=ALU.mult),
                       reads=[f"psc{j}"], writes=[f"psc{j}"])
                S.emit("pool", lambda e, j=j: e.tensor_tensor(out=psc[:PT, j, 2:3], in0=psc[:PT, j, 0:1], in1=psc[:PT, j, 0:1], op=ALU.mult),
                       reads=[f"psc{j}"], writes=[f"psc{j}"])
                S.emit("pool", lambda e, j=j: e.tensor_scalar(out=psc[:PT, j, 3:4], in0=psc[:PT, j, 1:2], scalar1=LN_EPS, scalar2=None, op0=ALU.add),
                       reads=[f"psc{j}"], writes=[f"psc{j}"])
                S.emit("pool", lambda e, j=j: e.tensor_tensor(out=psc[:PT, j, 3:4], in0=psc[:PT, j, 3:4], in1=psc[:PT, j, 2:3], op=ALU.subtract),
                       reads=[f"psc{j}"], writes=[f"psc{j}"])
                rsqrt_pool(sm[1][:PT, j:j + 1], psc[:PT, j, 3:4], [f"psc{j}"], f"pl1_{j}", (PT, 1))

            def wo_back(j):
                xr, xkey = xk[j]
                S.emit("dve", lambda e, xr=xr, j=j: e.scalar_tensor_tensor(
                    out=xr[:PT, :], in0=xr[:PT, :], scalar=psc[:PT, j, 0:1], in1=pg[l][:PT, :], op0=ALU.subtract, op1=ALU.mult),
                    reads=[xkey, f"psc{j}", f"pg{l}"], writes=[xkey])
                S.emit("dve", lambda e, xr=xr, j=j: e.scalar_tensor_tensor(
                    out=xr[:PT, :], in0=xr[:PT, :], scalar=sm[1][:PT, j:j + 1], in1=pbb[l][:PT, :], op0=ALU.mult, op1=ALU.add),
                    reads=[xkey, f"pl1_{j}", f"pbb{l}"], writes=[xkey])
                if y_out is not None:
                    S.emit("sp", lambda e, xr=xr, j=j: e.dma_start(out=y_out[j], in_=xr[:PT, :]), reads=[xkey], dma="d_o" + xkey[-1])

            for j in range(nj):
                wo_front(j)
                if j >= 1:
                    wo_back(j - 1)
            wo_back(nj - 1)

        tiles_per_seq = seq_len // TT
        n_tiles = n_seq * tiles_per_seq
        n_tl = n_tiles + (0 if "nosample" in _DBG else 1)
        for t in range(n_tl):
            for l in range(DEPTH):
                plan.extend((l, b, t == 0) for b in range(8))

        def chunk_idx(t, j):
            return (4 * t + j) % 7

        def emit_xload(t, js):
            b = t // tiles_per_seq
            ti = t % tiles_per_seq
            for j in js:
                idx = chunk_idx(t, j)
                S.emit("sp", lambda e, idx=idx, b=b, ti=ti, j=j: e.dma_start(
                    out=xres[idx][:, :], in_=xp[b, ti * TT + 128 * j:ti * TT + 128 * j + 128, :]),
                    writes=[f"xres{idx}"], dma=f"d_x{idx}")

        emit_xload(0, [0, 1, 2, 3])
        sidx = chunk_idx(n_tiles, 0)
        for t in range(n_tiles):
            b = t // tiles_per_seq
            ti = t % tiles_per_seq
            xk = [(xres[chunk_idx(t, j)], f"xres{chunk_idx(t, j)}") for j in range(4)]
            if t + 1 < n_tiles:
                emit_xload(t + 1, [0, 1, 2])
            else:
                S.emit("sp", lambda e: e.dma_start(out=xres[sidx][0:n_samp * DEC_SEQ, :], in_=xs.rearrange("b t d -> (b t) d")),
                       writes=[f"xres{sidx}"], dma=f"d_x{sidx}")
                for l in range(DEPTH):
                    c = chunk_idx(n_tiles, 1 + l)
                    av_chunk[l] = c
                    S.emit("sp", [lambda e, c=c, l=l: e.dma_start(out=xres[c][:, 0:512], in_=a_ln_g[l].partition_broadcast(128)),
                                  lambda e, c=c, l=l: e.dma_start(out=xres[c][:, 512:1024], in_=a_ln_b[l].partition_broadcast(128))],
                           writes=[f"xres{c}"], dma=f"d_x{c}")
            for l in range(DEPTH):
                tile_layer(l, TT, [(0, TT)], xk, ti == 0, ti == tiles_per_seq - 1, False, None,
                           [ncp[l, b]],
                           [yp[b, ti * TT + 128 * j:ti * TT + 128 * j + 128, :] for j in range(4)] if l == DEPTH - 1 else None,
                           None)
            if t + 1 < n_tiles:
                emit_xload(t + 1, [3])

        xr0 = (xres[sidx], f"xres{sidx}")
        ssegs = [(DEC_SEQ * s, DEC_SEQ) for s in range(n_samp)]
        for l in range(DEPTH if "nosample" not in _DBG else 0):
            tile_layer(l, n_samp * DEC_SEQ, ssegs, [xr0], False, True, True, 0,
                       [ncs[l, s] for s in range(n_samp)],
                       [ys.rearrange("b t d -> (b t) d")] if l == DEPTH - 1 else None,
                       [nav[l, s] for s in range(n_samp)])

        S.wait_all("sp", ["d_o0", "d_o1", "d_o2", "d_o3", "d_o4", "d_o5", "d_o6", "d_x5", "d_x6", "d_oc", "d_ov", "d_ws0", "d_ws1", "d_ws2", "d_ws3", "d_m", "d_m2", "d_s", "d_x0", "d_x1", "d_x2", "d_x3", "d_x4", "d_x5", "d_x6", "d_o5", "d_o6", "d_c", "d_pm"])

        with nc.allow_non_contiguous_dma(reason="tiny parameter gathers in the prologue"):
            with nc.Block() as block:
                @block.tensor
                def _(e):
                    S.replay("pe", e)

                @block.scalar
                def _(e):
                    S.replay("act", e)

                @block.vector
                def _(e):
                    S.replay("dve", e)

                @block.gpsimd
                def _(e):
                    S.replay("pool", e)

                @block.sync
                def _(e):
                    S.replay("sp", e)
    return nc


_CONSTS = None


def _consts():
    global _CONSTS
    if _CONSTS is None:
        k = np.arange(128)
        _CONSTS = {
            "c_ident": np.eye(128, dtype=np.float32),
            "c_tril": (k[:, None] <= k[None, :]).astype(np.float32),
            "c_i32": (k[:, None] % 32 == np.arange(32)[None, :]).astype(np.float32),
        }
    return _CONSTS


_NC_CACHE = {}


def kernel(x_prompt, x_sample, cache_conv, w_in, a_ln_g, a_ln_b, a_ws, a_bias, b_conv_w, b_conv_b,
           b_ln_g, b_ln_b, w_out, post_ln_g, post_ln_b):
    f = lambda a: np.ascontiguousarray(np.asarray(a, dtype=np.float32))
    x_prompt, x_sample, cache_conv = f(x_prompt), f(x_sample), f(cache_conv)
    shared = {"w_in": f(w_in), "w_out": f(w_out), "a_ln_g": f(a_ln_g), "a_ln_b": f(a_ln_b), "a_ws": f(a_ws),
              "a_bias": f(a_bias), "b_conv_w": f(b_conv_w), "b_conv_b": f(b_conv_b), "b_ln_g": f(b_ln_g),
              "b_ln_b": f(b_ln_b), "post_ln_g": f(post_ln_g), "post_ln_b": f(post_ln_b)}
    shared.update(_consts())
    nb, seq_len = x_prompt.shape[0], x_prompt.shape[1]
    n_seq = nb // NCORES
    n_samp = x_sample.shape[0] // NCORES
    key = (n_seq, seq_len, n_samp)
    if key not in _NC_CACHE:
        _NC_CACHE[key] = build_nc(n_seq, seq_len, n_samp)
    nc = _NC_CACHE[key]
    in_maps = []
    for c in range(NCORES):
        m = dict(shared)
        m["xp"] = np.ascontiguousarray(x_prompt[c * n_seq:(c + 1) * n_seq])
        m["xs"] = np.ascontiguousarray(x_sample[c * n_samp:(c + 1) * n_samp])
        m["cc"] = np.ascontiguousarray(cache_conv[:, c * n_samp:(c + 1) * n_samp])
        in_maps.append(m)
    res = run_bass_kernel_spmd(nc, in_maps, core_ids=list(range(NCORES)))
    R = res.results
    y_p = np.concatenate([r["yp"] for r in R], axis=0)
    y_s = np.concatenate([r["ys"] for r in R], axis=0)
    ncp = np.concatenate([r["ncp"] for r in R], axis=1)
    ncs = np.concatenate([r["ncs"] for r in R], axis=1)
    nav = np.concatenate([r["nav"] for r in R], axis=1)
    return (y_p.astype(np.float32), y_s.astype(np.float32), ncp.astype(np.float32), ncs.astype(np.float32),
            nav.astype(np.float32))
```

```python
import contextlib
import numpy as np
import concourse.bass as bass
import concourse.mybir as mybir
from concourse.bass_utils import run_bass_kernel_spmd

F32 = mybir.dt.float32
BF16 = mybir.dt.bfloat16
I32 = mybir.dt.int32
AF = mybir.ActivationFunctionType
ALU = mybir.AluOpType

NCORES = 8
D = 1024
DEPTH = 2
SEQ = 2048
BATCH = 32
DEC_BATCH = 16
DEC_SEQ = 32
WA = 512
CONV_K = 31
HIST = 30
ALPHA = float((2 * DEPTH) ** 0.25)
LN_EPS = 1e-5
TT = 512
NSLOT = 4
BLOCKS = [("in", 2048), ("in", 1536), ("in", 2560), ("in", 512), ("in", 1024), ("in", 0),
          ("out", 0), ("out", 512)]
B_BG, B_BV, B_GB, B_V, B_GA, B_U, B_O0, B_O1 = range(8)


class _Eng:
    def __init__(self, name, sem):
        self.name = name
        self.sem = sem
        self.count = 0
        self.ops = []
        self.waited = {}


class Sched:
    def __init__(self):
        self.engs = {}
        self.last_w = {}
        self.readers = {}

    def add_engine(self, name, sem):
        self.engs[name] = _Eng(name, sem)

    def _deps(self, reads, writes):
        deps = {}

        def add(e, v):
            if v > deps.get(e, 0):
                deps[e] = v
        for b in reads:
            lw = self.last_w.get(b)
            if lw:
                add(*lw)
        for b in writes:
            lw = self.last_w.get(b)
            if lw:
                add(*lw)
            for e, v in self.readers.get(b, {}).items():
                add(e, v)
        return deps

    def emit(self, eng, fns, reads=(), writes=(), dma=None):
        E = self.engs[eng]
        if not isinstance(fns, (list, tuple)):
            fns = [fns]
        deps = self._deps(reads, writes)
        done = self.engs[dma] if dma else E
        for e, v in deps.items():
            if e == eng and eng == "pe":
                continue
            if v <= E.waited.get(e, 0):
                continue
            E.waited[e] = v
            E.ops.append(("wait", (e, v)))
        if dma:
            for fn in fns:
                done.count += 16
                E.ops.append(("dma", (fn, dma)))
        else:
            for fn in fns[:-1]:
                E.ops.append(("op", fn))
            done.count += 1
            E.ops.append(("opinc", fns[-1]))
        val = done.count
        dn = done.name
        for b in writes:
            self.last_w[b] = (dn, val)
            self.readers[b] = {}
        for b in reads:
            r = self.readers.setdefault(b, {})
            if val > r.get(dn, 0):
                r[dn] = val

    def settle(self, dma, keys):
        c = self.engs[dma].count
        for k in keys:
            self.last_w[k] = (dma, c)

    def wait_all(self, eng, targets):
        E = self.engs[eng]
        for t in targets:
            T = self.engs[t]
            if T.count > E.waited.get(t, 0):
                E.waited[t] = T.count
                E.ops.append(("wait", (t, T.count)))

    def replay(self, eng, handle):
        E = self.engs[eng]
        for kind, p in E.ops:
            if kind == "wait":
                e, v = p
                handle.wait_ge(self.engs[e].sem, v)
            elif kind == "op":
                p(handle)
            elif kind == "opinc":
                p(handle).then_inc(E.sem, 1)
            else:
                fn, d = p
                fn(handle).then_inc(self.engs[d].sem, 16)


import os
_DBG = set(os.environ.get("KDEBUG", "").split(","))


def build_nc(n_seq=4, seq_len=SEQ, n_samp=2):
    assert seq_len % TT == 0
    nc = bass.Bass("TRN2", target_bir_lowering=False)
    dr = lambda name, shape, dt=F32, kind="ExternalInput": nc.dram_tensor(name, shape, dt, kind=kind).ap()
    xp = dr("xp", [n_seq, seq_len, D])
    xs = dr("xs", [n_samp, DEC_SEQ, D])
    cc = dr("cc", [DEPTH, n_samp, HIST, WA])
    w_in = dr("w_in", [DEPTH, D, 3072])
    w_out = dr("w_out", [DEPTH, D, D])
    a_ln_g = dr("a_ln_g", [DEPTH, WA])
    a_ln_b = dr("a_ln_b", [DEPTH, WA])
    a_ws = dr("a_ws", [DEPTH, 8, 128, 128])
    a_bias = dr("a_bias", [DEPTH, 8, 128])
    b_conv_w = dr("b_conv_w", [DEPTH, CONV_K, WA])
    b_conv_b = dr("b_conv_b", [DEPTH, WA])
    b_ln_g = dr("b_ln_g", [DEPTH, WA])
    b_ln_b = dr("b_ln_b", [DEPTH, WA])
    post_ln_g = dr("post_ln_g", [DEPTH, D])
    post_ln_b = dr("post_ln_b", [DEPTH, D])
    c_ident = dr("c_ident", [128, 128])
    c_tril = dr("c_tril", [128, 128])
    c_i32 = dr("c_i32", [128, 32])
    yp = dr("yp", [n_seq, seq_len, D], kind="ExternalOutput")
    ys = dr("ys", [n_samp, DEC_SEQ, D], kind="ExternalOutput")
    ncp = dr("ncp", [DEPTH, n_seq, HIST, WA], kind="ExternalOutput")
    ncs = dr("ncs", [DEPTH, n_samp, HIST, WA], kind="ExternalOutput")
    nav = dr("nav", [DEPTH, n_samp, DEC_SEQ, WA], kind="ExternalOutput")
    wsc = dr("wsc", [DEPTH, 8, 128, 8 * 512], BF16, kind="Internal")

    es = contextlib.ExitStack()
    with es:
        sb = lambda name, shape, dt=F32: es.enter_context(nc.sbuf_tensor(name, shape, dt))
        S = Sched()
        for n in ["pe", "act", "dve", "pool", "sp", "d_c", "d_w0", "d_w1", "d_w2", "d_w3",
                  "d_s", "d_o0", "d_o1", "d_o2", "d_o3", "d_o4", "d_o5", "d_o6", "d_x5", "d_x6", "d_mb", "d_m3a", "d_m3b", "d_oc", "d_ov", "d_m", "d_pw0", "d_pw1", "d_pw2", "d_pw3", "d_pm", "d_ws0", "d_ws1", "d_ws2", "d_ws3", "d_x0", "d_x1", "d_x2", "d_x3", "d_x4", "d_m2"]:
            S.add_engine(n, es.enter_context(nc.semaphore("s_" + n)))

        xres = [sb(f"xres{i}", [128, D]) for i in range(7)]
        yT = sb("yT", [128, 8, TT], BF16)
        xbf = yT[:].rearrange("p k t -> p (k t)").rearrange("p (j d) -> p j d", d=D)
        xT = sb("xT", [128, 8, TT], BF16)
        tmpf = [sb(f"tmpf{i}", [128, TT]) for i in range(4)]
        tTt = sb("tT", [128, 4, TT])
        vn = sb("vn", [128, 4, WA], BF16)
        a2T = [sb(f"a2T{l}", [128, 4, 544], BF16) for l in range(DEPTH)]
        Sb = sb("Sb", [128, 16, 540], BF16)
        sgb = sb("sgb", [128, 4, TT])
        cT = sb("cT", [128, 4, TT])
        cbf = sb("cbf", [128, 4, TT], BF16)
        csq = sb("csq", [128, 4, TT], BF16)
        mean_sb = sb("mean_sb", [128, TT])
        rstd_b = sb("rstd_b", [128, TT])
        nt_a = sb("nt_a", [128, TT])
        alast = sb("alast", [128, 4, 2, 32])
        al_p = [sb(f"al_p{i}", [128, 4, 2, 32], BF16) for i in range(3)]
        al_r = sb("al_r", [128, 4, 2, 32])
        vf32 = sb("ostage", [64, WA])
        ostage = vf32
        mhalf = sb("mhalf", [128, 8])
        chbf = sb("chbf", [32, 2, WA], BF16)
        st6 = sb("st6", [128, 8, 6])
        st6v = sb("st6v", [128, 4, 6])
        one1 = sb("one1", [128, 2])
        junk1 = sb("junk1", [128, 2])
        pst = sb("pst", [128, 4, 2, 2])
        psc = sb("psc", [128, 4, 4])
        mv = sb("mv", [128, 4, 2])
        sm = [sb(f"sm{i}", [128, 4]) for i in range(5)]
        wr = [sb(f"wr{i}", [128, 8, 512], BF16) for i in range(NSLOT)]
        Wst = [sb(f"Wst{l}", [128, 16, 8, 32], BF16) for l in range(DEPTH)]
        wTt = sb("wTt", [128, 16, 8])
        WmT = [sb(f"WmT{l}", [128, 8, 128], BF16) for l in range(DEPTH)]
        WmTs = [sb(f"WmTs{l}", [64, 2, 8, 32], BF16) for l in range(DEPTH)]
        Bt = [sb(f"Bt{l}", [128, 4, 128]) for l in range(DEPTH)]
        pg = [sb(f"pg{l}", [128, D]) for l in range(DEPTH)]
        pbb = [sb(f"pbb{l}", [128, D]) for l in range(DEPTH)]
        av_chunk = [5, 6]
        agc = [sb(f"agc{l}", [128, 4]) for l in range(DEPTH)]
        cbc = [sb(f"cbc{l}", [128, 4]) for l in range(DEPTH)]
        bgc = [sb(f"bgc{l}", [128, 4]) for l in range(DEPTH)]
        bbc = [sb(f"bbc{l}", [128, 4]) for l in range(DEPTH)]
        ident_f = sb("ident_f", [128, 128])
        identb = sb("identb", [128, 128], BF16)
        tril_f = sb("tril_f", [128, 128])
        trilb = sb("trilb", [128, 128], BF16)
        i32m = sb("i32m", [128, 32])
        onesb = sb("onesb", [128, 128], BF16)
        wsq = cT[:].rearrange("p m t -> p (m t)")[:, 0:1024].rearrange("p (h k) -> p h k", k=128)
        wsqb = cbf[:].rearrange("p m t -> p (m t)")[:, 0:1024].rearrange("p (h k) -> p h k", k=128)
        vbb = csq[:].rearrange("p m t -> p (m t)")[:, 0:WA]
        abias_bc = sgb[:].rearrange("p m t -> p (m t)")[:, 0:1024].rearrange("p (h q) -> p h q", q=128)

        banks = [es.enter_context(nc.psum_tensor(f"pb{i}", [128, 512], F32)) for i in range(6)]
        ptrs = [es.enter_context(nc.psum_tensor(f"ptr{i}", [128, 1024], BF16)) for i in range(2)]

        class _Ptr:
            def __getitem__(self, idx):
                p, i, c = idx
                return ptrs[i][p, c]
        ptr = _Ptr()
        bank_i = [0]

        def next_bank():
            i = bank_i[0] % 6
            bank_i[0] += 1
            return banks[i], f"pb{i}"
        ptr_i = [0]

        def next_ptr():
            i = ptr_i[0] % 2
            ptr_i[0] += 1
            return i, f"ptr{i}"
        tmp_i = [0]

        def next_tmp():
            i = tmp_i[0] % 4
            tmp_i[0] += 1
            return tmpf[i], f"tmpf{i}"

        def rsqrt_pool(y, x, reads, ykey, shape):
            ex = mhalf[:shape[0], 0:shape[1]]
            S.emit("pool", lambda e: e.tensor_tensor(out=y, in0=x, in1=ex, op=ALU.pow), reads=list(reads) + ["mhalf"], writes=[ykey])

        def rsqrt_dve(y, x, ta, xh, reads, ykey, iters=3, takey=None, xhkey=None):
            keys = [ykey, takey or (ykey + "_ta"), xhkey or (ykey + "_xh")]
            S.emit("dve", lambda e: e.tensor_scalar(out=y.bitcast(I32), in0=x.bitcast(I32), scalar1=1, scalar2=None,
                                                    op0=ALU.arith_shift_right), reads=reads, writes=[keys[0]])
            S.emit("dve", lambda e: e.tensor_scalar(out=y.bitcast(I32), in0=y.bitcast(I32), scalar1=-1,
                                                    scalar2=0x5f3759df, op0=ALU.mult, op1=ALU.add),
                   reads=[keys[0]], writes=[keys[0]])
            S.emit("dve", lambda e: e.tensor_scalar(out=xh, in0=x, scalar1=-0.5, scalar2=None, op0=ALU.mult),
                   reads=reads, writes=[keys[2]])
            for _ in range(iters):
                S.emit("dve", lambda e: e.tensor_tensor(out=ta, in0=y, in1=y, op=ALU.mult), reads=[keys[0]], writes=[keys[1]])
                S.emit("dve", lambda e: e.tensor_tensor(out=ta, in0=ta, in1=xh, op=ALU.mult), reads=[keys[1], keys[2]],
                       writes=[keys[1]])
                S.emit("dve", lambda e: e.scalar_tensor_tensor(out=y, in0=ta, scalar=1.5, in1=y, op0=ALU.add, op1=ALU.mult),
                       reads=[keys[0], keys[1]], writes=[keys[0]])

        CT4 = [f"cT{m}" for m in range(4)]
        SGB4 = [f"sgb{m}" for m in range(4)]
        CBF4 = [f"cbf{m}" for m in range(4)]
        CSQ4 = [f"csq{m}" for m in range(4)]
        pro = []

        def pload(dst_ap, src_ap, key, eng="sp"):
            S.emit(eng, lambda e: e.dma_start(out=dst_ap, in_=src_ap), writes=[key], dma="d_c")
            pro.append(key)
        pload(ident_f[:], c_ident, "ident_f")
        pload(tril_f[:], c_tril, "tril_f")
        pload(i32m[:], c_i32, "i32m")
        for l in range(DEPTH):
            pload(pg[l][:], post_ln_g[l].partition_broadcast(128), f"pg{l}")
            pload(pbb[l][:], post_ln_b[l].partition_broadcast(128), f"pbb{l}")
            pload(xres[av_chunk[l]][:, 512:1024], a_ln_b[l].partition_broadcast(128), f"xres{av_chunk[l]}")
            pload(agc[l][:], a_ln_g[l].rearrange("(m p) -> p m", p=128), f"agc{l}")
            pload(cbc[l][:], b_conv_b[l].rearrange("(m p) -> p m", p=128), f"cbc{l}")
            pload(bgc[l][:], b_ln_g[l].rearrange("(m p) -> p m", p=128), f"bgc{l}")
            pload(bbc[l][:], b_ln_b[l].rearrange("(m p) -> p m", p=128), f"bbc{l}")
        S.settle("d_c", pro)
        S.emit("dve", lambda e: e.tensor_copy(out=identb[:], in_=ident_f[:]), reads=["ident_f"], writes=["identb"])
        S.emit("dve", lambda e: e.tensor_copy(out=trilb[:], in_=tril_f[:]), reads=["tril_f"], writes=["trilb"])
        S.emit("pool", lambda e: e.memset(onesb[:], 1.0 / 512.0), writes=["onesb"])
        S.emit("pool", lambda e: e.memset(mhalf[:], -0.5), writes=["mhalf"])
        S.emit("pool", lambda e: e.memset(one1[:], 1.0), writes=["one1"])
        for l in range(DEPTH):
            S.emit("pool", lambda e, l=l: e.memset(a2T[l][:], 0.0), writes=[f"a2T{l}"])
        S.emit("pool", lambda e: e.memset(Sb[:], 0.0), writes=["Sb"])
        S.emit("pool", lambda e: e.memset(chbf[:], 0.0), writes=["chbf0", "chbf1"])

        TT4 = [f"tT{m}" for m in range(4)]
        scr = [
            dict(wsq=wsq, kq=CT4, wsqb=wsqb, kqb=CBF4, vbb=vbb, kv=CSQ4, ab=abias_bc, kab=SGB4, dm="d_m", dm3="d_m3a"),
            dict(wsq=tTt[:].rearrange("p m t -> p (m t)")[:, 0:1024].rearrange("p (h k) -> p h k", k=128), kq=TT4,
                 wsqb=yT[:].rearrange("p k t -> p (k t)")[:, 0:1024].rearrange("p (h k) -> p h k", k=128), kqb=["yT0", "yT1", "xbf0"],
                 vbb=xT[:, 0, :], kv=[f"xTj{j}" for j in range(4)],
                 ab=xres[4][:, :].rearrange("p (h q) -> p h q", q=128), kab=["xres4"], dm="d_mb", dm3="d_m3b"),
        ]
        for l in range(DEPTH):
            sc = scr[l]
            S.emit("sp", lambda e, l=l, sc=sc: e.dma_start(out=sc["wsq"], in_=a_ws[l].rearrange("h q k -> q h k")), writes=sc["kq"], dma=sc["dm"])
            S.emit("sp", lambda e, l=l, sc=sc: e.dma_start(out=sc["ab"], in_=a_bias[l].rearrange("h q -> (h q)").partition_broadcast(128)
                                                          .rearrange("p (h q) -> p h q", q=128)), writes=sc["kab"], dma=sc["dm"])
            S.settle(sc["dm"], sc["kq"] + sc["kab"])
        for l in range(DEPTH):
            sc = scr[l]
            S.emit("pool", lambda e: e.memset(wTt[:], 0.0), writes=["wTt"])
            cw = []
            for s in range(4):
                for mm in range(8):
                    j = 4 * mm + s
                    if j >= CONV_K:
                        continue
                    cw.append(lambda e, l=l, s=s, mm=mm, j=j: e.dma_start(
                        out=wTt[32 * s:32 * s + 32, :, mm], in_=b_conv_w[l, j, :].rearrange("(g c) -> c g", c=32)))
            S.emit("sp", cw, writes=["wTt"], dma=sc["dm3"])
            S.settle(sc["dm3"], ["wTt"])
            S.emit("dve", lambda e, sc=sc: e.tensor_copy(out=sc["wsqb"], in_=sc["wsq"]), reads=sc["kq"], writes=sc["kqb"])
            for hb in range(2):
                pi, pk = next_ptr()
                S.emit("pe", [(lambda e, h=h, pi=pi, sc=sc: e.transpose(out=ptr[:, pi, 128 * (h % 4):128 * (h % 4) + 128],
                                                                        in_=sc["wsqb"][:, h, :], identity=identb[:]))
                              for h in range(4 * hb, 4 * hb + 4)], reads=sc["kqb"] + ["identb"], writes=[pk])
                S.emit("dve", lambda e, l=l, hb=hb, pi=pi: e.tensor_tensor(
                    out=WmT[l][:, 4 * hb:4 * hb + 4, :], in0=ptr[:, pi, 0:512].rearrange("p (h q) -> p h q", q=128),
                    in1=trilb[:].unsqueeze(1).broadcast_to([128, 4, 128]), op=ALU.mult),
                    reads=[pk, "trilb"], writes=[f"WmT{l}"])
            S.emit("pool", lambda e, l=l: e.memset(WmTs[l][:], 0.0), writes=[f"WmTs{l}_0", f"WmTs{l}_1"])
            S.emit("sp", [(lambda e, l=l, s2=s2: e.dma_start(out=WmTs[l][32 * s2:32 * s2 + 32, s2, :, :], in_=WmT[l][0:32, :, 0:32]))
                          for s2 in range(2)], reads=[f"WmT{l}"], writes=[f"WmTs{l}_0", f"WmTs{l}_1"], dma="d_m2")
            S.emit("dve", lambda e, c=av_chunk[l], sc=sc: e.tensor_copy(out=sc["vbb"], in_=xres[c][:, 512:1024]),
                   reads=[f"xres{av_chunk[l]}"], writes=sc["kv"])
            pbk, pkk = next_bank()
            S.emit("pe", [(lambda e, l=l, m=m, hh=hh, pbk=pbk, sc=sc: e.matmul(
                pbk[64 * hh:64 * hh + 64, 128 * m:128 * m + 128], lhsT=sc["vbb"][:, 64 * (2 * m + hh):64 * (2 * m + hh) + 64],
                rhs=WmT[l][:, 2 * m + hh, :], start=True, stop=True, tile_position=(0, 64 * hh)))
                for m in range(4) for hh in range(2)], reads=sc["kv"] + [f"WmT{l}"], writes=[pkk])
            ab4 = sc["ab"].rearrange("p (m two) q -> p m two q", two=2)
            for hh in range(2):
                S.emit("dve", lambda e, l=l, hh=hh, pbk=pbk, ab4=ab4: e.tensor_tensor(
                    out=Bt[l][64 * hh:64 * hh + 64, :, :],
                    in0=pbk[64 * hh:64 * hh + 64, :].rearrange("p (m q) -> p m q", q=128),
                    in1=ab4[64 * hh:64 * hh + 64, :, hh, :], op=ALU.add),
                    reads=[pkk] + sc["kab"], writes=[f"Bt{l}_{hh}"])
            S.emit("dve", lambda e, l=l: e.scalar_tensor_tensor(
                out=Wst[l][:].rearrange("p g m c -> p (g m) c"),
                in0=wTt[:].rearrange("p g m -> p (g m)").unsqueeze(2).broadcast_to([128, 128, 32]), scalar=0.5,
                in1=i32m[:].unsqueeze(1).broadcast_to([128, 128, 32]), op0=ALU.mult, op1=ALU.mult),
                reads=["wTt", "i32m"], writes=[f"Wst{l}"])

        S.settle("d_m2", [f"WmTs{l}_{s}" for l in range(DEPTH) for s in range(2)])

        gblk = [0]
        wnext = [0]
        plan = []

        def w_src(l, b):
            kind, c0 = BLOCKS[b]
            src = (w_in if kind == "in" else w_out)[l]
            return src[:, c0:c0 + 512].rearrange("(k p) e -> p k e", p=128)

        def w_prefetch(upto):
            while wnext[0] <= upto and wnext[0] < len(plan):
                g = wnext[0]
                l, b, first = plan[g]
                slot = g % NSLOT
                if first:
                    S.emit("pool", lambda e, l=l, b=b, slot=slot: e.dma_start(out=wr[slot][:], in_=w_src(l, b)),
                           writes=[f"wr{slot}"], dma=f"d_pw{slot}")
                    S.emit("sp", lambda e, l=l, b=b, slot=slot: e.dma_start(
                        out=wsc[l, b], in_=wr[slot][:].rearrange("p k e -> p (k e)")),
                        reads=[f"wr{slot}"], writes=[f"wsc{l}_{b}"], dma=f"d_ws{slot}")
                else:
                    S.emit("sp", lambda e, l=l, b=b, slot=slot: e.dma_start(
                        out=wr[slot][:].rearrange("p k e -> p (k e)"), in_=wsc[l, b]),
                        reads=[f"wsc{l}_{b}"], writes=[f"wr{slot}"], dma=f"d_w{slot}")
                wnext[0] += 1

        def w_block(prefetch=True):
            g = gblk[0]
            gblk[0] += 1
            if prefetch:
                w_prefetch(g + NSLOT - 1)
            slot = g % NSLOT
            return wr[slot], f"wr{slot}"

        def tile_layer(l, T, segs, xk, first_in_seq, last_in_seq, sample, hist_src, conv_out, y_out, av_out):
            nj = len(xk)
            PT = min(T, 128)
            nseg = len(segs)
            TS = segs[0][1]
            if sample:
                a2v = a2T[l][:, :, 0:128].rearrange("p m (s c) -> p m s c", c=64)
                Sv = Sb[:, :, 0:120].rearrange("p g (s c) -> p g s c", c=60)
            else:
                a2v = a2T[l][:, :, :].unsqueeze(2)
                Sv = Sb[:, :, :].unsqueeze(2)
            ka2 = f"a2T{l}"
            for j in range(nj):
                S.emit("act", lambda e, j=j: e.copy(out=xbf[:PT, j, :], in_=xk[j][0][:PT, :]),
                       reads=[xk[j][1]], writes=[f"xbf{j}", f"yT{2 * j}", f"yT{2 * j + 1}"])
                pi, pk = next_ptr()
                S.emit("pe", [(lambda e, j=j, k=k, pi=pi: e.transpose(
                    out=ptr[:, pi, 128 * k:128 * k + PT], in_=xbf[:PT, j, 128 * k:128 * k + 128], identity=identb[:PT, :PT]))
                    for k in range(8)], reads=[f"xbf{j}", "identb"], writes=[pk])
                eng = "act" if j % 2 == 1 else "dve"
                src_v = lambda pi=pi: ptr[:, pi, 0:1024].rearrange("p (k t) -> p k t", t=128)[:, :, 0:PT]
                if eng == "act":
                    S.emit("act", lambda e, j=j, src_v=src_v: e.copy(out=xT[:, :, 128 * j:128 * j + PT], in_=src_v()),
                           reads=[pk], writes=[f"xTj{j}"])
                else:
                    S.emit("dve", lambda e, j=j, src_v=src_v: e.tensor_copy(out=xT[:, :, 128 * j:128 * j + PT], in_=src_v()),
                           reads=[pk], writes=[f"xTj{j}"])
            xTk = [f"xTj{j}" for j in range(nj)]

            def fm_block(consume):
                wt, wk = w_block()
                for m in range(4):
                    pb, pk = next_bank()
                    S.emit("pe", [(lambda e, k=k, m=m, pb=pb, wt=wt: e.matmul(
                        pb[:, 0:T], lhsT=wt[:, k, 128 * m:128 * m + 128], rhs=xT[:, k, 0:T], start=(k == 0), stop=(k == 7)))
                        for k in range(8)], reads=[wk] + xTk, writes=[pk])
                    consume(m, pb, pk)

            def fm_block_split(consume):
                wt, wk = w_block()
                jobs = [next_bank() for _ in range(4)]
                for half in range(2):
                    c0 = 256 * half
                    for m in range(4):
                        pb, pk = jobs[m]
                        S.emit("pe", [(lambda e, k=k, m=m, pb=pb, wt=wt, c0=c0: e.matmul(
                            pb[:, c0:c0 + 256], lhsT=wt[:, k, 128 * m:128 * m + 128], rhs=xT[:, k, c0:c0 + 256],
                            start=(k == 0), stop=(k == 7))) for k in range(8)],
                            reads=[wk, f"xTj{2 * half}", f"xTj{2 * half + 1}"], writes=[pk])
                for m in range(4):
                    consume(m, jobs[m][0], jobs[m][1])

            if hist_src is not None:
                for s in range(nseg):
                    S.emit("pool", lambda e, s=s: e.dma_start(out=chbf[0:HIST, s, :], in_=cc[l, hist_src + s]),
                           writes=[f"chbf{s}"], dma="d_pm")
                S.settle("d_pm", [f"chbf{s}" for s in range(nseg)])
                for s in range(nseg):
                    pi, pk = next_ptr()
                    S.emit("pe", [(lambda e, m=m, s=s, pi=pi: e.transpose(
                        out=ptr[:, pi, 32 * m:32 * m + 32], in_=chbf[0:32, s, 128 * m:128 * m + 128],
                        identity=identb[:32, :32])) for m in range(4)], reads=[f"chbf{s}", "identb"], writes=[pk])
                    S.emit("act", lambda e, s=s, pi=pi: e.mul(
                        out=a2v[:, :, s, 0:HIST], in_=ptr[:, pi, 0:128].rearrange("p (m c) -> p m c", c=32)[:, :, 0:HIST], mul=2.0),
                        reads=[pk], writes=[ka2])
            elif first_in_seq:
                S.emit("pool", lambda e: e.memset(a2T[l][:, :, 0:HIST], 0.0), writes=[ka2])
            else:
                S.emit("pool", lambda e: e.tensor_copy(out=a2T[l][:, :, 0:HIST], in_=a2T[l][:, :, TT:TT + HIST]),
                       reads=[ka2], writes=[ka2])

            ths = []

            def c_bg(m, pb, pk):
                tp, tk = next_tmp()
                ths.append((tp, tk))
                S.emit("act", lambda e: e.activation(out=tp[:, 0:T], in_=pb[:, 0:T], func=AF.Tanh, scale=0.5),
                       reads=[pk], writes=[tk])
            (fm_block_split if nj == 4 else fm_block)(c_bg)

            def c_bv(m, pb, pk):
                tp, tk = ths[m]
                fns = [(lambda e, s=s, c0=c0, n=n: e.scalar_tensor_tensor(
                    out=a2v[:, m, s, HIST:HIST + n], in0=tp[:, c0:c0 + n], scalar=1.0, in1=pb[:, c0:c0 + n],
                    op0=ALU.add, op1=ALU.mult)) for s, (c0, n) in enumerate(segs)]
                wr_keys = [ka2]
                if last_in_seq:
                    fns += [(lambda e, s=s, c0=c0, n=n: e.scalar_tensor_tensor(
                        out=alast[:, m, s, :], in0=tp[:, c0 + n - 32:c0 + n], scalar=1.0, in1=pb[:, c0 + n - 32:c0 + n],
                        op0=ALU.add, op1=ALU.mult)) for s, (c0, n) in enumerate(segs)]
                    wr_keys.append("alast")
                S.emit("dve", fns, reads=[tk, pk], writes=wr_keys)
            fm_block(c_bv)

            ncol = TS + 28
            S.emit("sp", [(lambda e, s=s, g=g, sg=sg: e.dma_start(
                out=Sv[32 * s:32 * s + 32, :, sg, 0:ncol].rearrange("p (m f) c -> p m f c", f=4)[:, :, g, :],
                in_=a2v[32 * g:32 * g + 32, :, sg, s:s + ncol])) for s in range(4) for g in range(4) for sg in range(nseg)],
                reads=[ka2], writes=["Sb"], dma="d_s")

            if last_in_seq and "noconvout" not in _DBG:
                S.emit("dve", lambda e: e.tensor_copy(out=al_p[0][:, :, 0:nseg, :], in_=alast[:, :, 0:nseg, :]), reads=["alast"], writes=["al_p0"])
                S.emit("dve", lambda e: e.tensor_tensor(out=al_r[:, :, 0:nseg, :], in0=alast[:, :, 0:nseg, :], in1=al_p[0][:, :, 0:nseg, :],
                                                        op=ALU.subtract), reads=["alast", "al_p0"], writes=["al_r"])
                S.emit("dve", lambda e: e.tensor_copy(out=al_p[1][:, :, 0:nseg, :], in_=al_r[:, :, 0:nseg, :]), reads=["al_r"], writes=["al_p1"])
                S.emit("dve", lambda e: e.tensor_tensor(out=al_r[:, :, 0:nseg, :], in0=al_r[:, :, 0:nseg, :], in1=al_p[1][:, :, 0:nseg, :],
                                                        op=ALU.subtract), reads=["al_r", "al_p1"], writes=["al_r"])
                S.emit("dve", lambda e: e.tensor_copy(out=al_p[2][:, :, 0:nseg, :], in_=al_r[:, :, 0:nseg, :]), reads=["al_r"], writes=["al_p2"])
                for s in range(nseg):
                    pb, pk = next_bank()
                    S.emit("pe", [(lambda e, m=m, i=i, s=s, pb=pb: e.matmul(
                        pb[0:32, 128 * m:128 * m + 128], lhsT=al_p[i][:, m, s, :], rhs=identb[:], start=(i == 0), stop=(i == 2)))
                        for m in range(4) for i in range(3)], reads=["al_p0", "al_p1", "al_p2", "identb"], writes=[pk])
                    S.emit("act", lambda e, pb=pb: e.mul(out=ostage[0:32, :], in_=pb[0:32, :], mul=0.5), reads=[pk], writes=["ostage"])
                    S.emit("sp", lambda e, s=s: e.dma_start(out=conv_out[s], in_=ostage[2:32, :]), reads=["ostage"], dma="d_oc")

            def c_gb(m, pb, pk):
                S.emit("act", lambda e: e.activation(out=sgb[:, m, 0:T], in_=pb[:, 0:T], func=AF.Silu), reads=[pk], writes=[f"sgb{m}"])
            fm_block(c_gb)

            wt, wk = w_block()
            gvs = []
            for j in range(nj):
                pb, pk = next_bank()
                S.emit("pe", [(lambda e, k=k, j=j, pb=pb, wt=wt: e.matmul(
                    pb[:PT, :], lhsT=xT[:, k, 128 * j:128 * j + PT], rhs=wt[:, k, :], start=(k == 0), stop=(k == 7)))
                    for k in range(8)], reads=[wk] + xTk, writes=[pk])
                tp, tk = next_tmp()
                gvs.append((tp, tk))
                S.emit("act", lambda e, pb=pb, tp=tp: e.activation(out=tp[:PT, :], in_=pb[:PT, :], func=AF.Gelu), reads=[pk], writes=[tk])
                S.emit("dve", lambda e, j=j, tp=tp: e.bn_stats(out=st6v[:PT, j, :], in_=tp[:PT, :]), reads=[tk], writes=[f"st6v_{j}"])
                S.emit("dve", lambda e, j=j: e.bn_aggr(out=mv[:PT, j, :], in_=st6v[:PT, j, :]), reads=[f"st6v_{j}"], writes=[f"mv{j}"])
            mvk = [f"mv{j}" for j in range(nj)]
            S.emit("pool", lambda e: e.tensor_scalar(out=sm[0][:PT, 0:nj], in0=mv[:PT, 0:nj, 1], scalar1=LN_EPS, scalar2=None, op0=ALU.add),
                   reads=mvk, writes=["sm0"])
            rsqrt_pool(sm[1][:PT, 0:nj], sm[0][:PT, 0:nj], ["sm0"], "sm1", (PT, nj))
            S.emit("dve", lambda e: e.scalar_tensor_tensor(out=sm[4][:PT, 0:nj], in0=mv[:PT, 0:nj, 0], scalar=-1.0, in1=sm[1][:PT, 0:nj],
                                                           op0=ALU.mult, op1=ALU.mult), reads=mvk + ["sm1"], writes=["sm4"])
            for j in range(nj):
                tp, tk = gvs[j]
                S.emit("act", lambda e, j=j, tp=tp: e.activation(out=vn[:PT, j, :], in_=tp[:PT, :], func=AF.Identity,
                                                                 scale=sm[1][:PT, j:j + 1], bias=sm[4][:PT, j:j + 1]),
                       reads=[tk, "sm1", "sm4"], writes=[f"vn{j}"])
                if sample:
                    S.emit("act", lambda e, j=j, tp=tp: e.activation(out=vf32[:PT, :], in_=tp[:PT, :], func=AF.Identity,
                                                                     scale=sm[1][:PT, j:j + 1], bias=sm[4][:PT, j:j + 1]),
                           reads=[tk, "sm1", "sm4"], writes=["ostage"])
                    avc = av_chunk[l]
                    S.emit("dve", lambda e, avc=avc: e.tensor_tensor(out=vf32[:PT, :], in0=vf32[:PT, :], in1=xres[avc][:PT, 0:512], op=ALU.mult),
                           reads=["ostage", f"xres{avc}"], writes=["ostage"])
                    S.emit("dve", lambda e, avc=avc: e.tensor_tensor(out=vf32[:PT, :], in0=vf32[:PT, :], in1=xres[avc][:PT, 512:1024], op=ALU.add),
                           reads=["ostage", f"xres{avc}"], writes=["ostage"])
                    S.emit("sp", [(lambda e, s=s, c0=c0, n=n: e.dma_start(out=av_out[s], in_=vf32[c0:c0 + n, :]))
                                  for s, (c0, n) in enumerate(segs)], reads=["ostage"], dma="d_ov")

            pbm, pkm = next_bank()
            pbq, pkq = next_bank()

            def stat_mm(m):
                S.emit("pe", lambda e, m=m: e.matmul(pbm[:, 0:T], lhsT=onesb[:], rhs=cbf[:, m, 0:T], start=(m == 0), stop=(m == 3)),
                       reads=["onesb", f"cbf{m}"], writes=[pkm])
                S.emit("pe", lambda e, m=m: e.matmul(pbq[:, 0:T], lhsT=onesb[:], rhs=csq[:, m, 0:T], start=(m == 0), stop=(m == 3)),
                       reads=["onesb", f"csq{m}"], writes=[pkq])

            for m in range(4):
                pb, pk = next_bank()
                fns = []
                for s, (c0, n) in enumerate(segs):
                    for mm in range(8):
                        for g in range(4):
                            fns.append(lambda e, m=m, s=s, c0=c0, n=n, mm=mm, g=g, pb=pb: e.matmul(
                                pb[32 * g:32 * g + 32, c0:c0 + n], lhsT=Wst[l][:, 4 * m + g, mm, :],
                                rhs=Sv[:, 4 * m + g, s, 4 * mm:4 * mm + n], start=(mm == 0), stop=(mm == 7),
                                tile_position=(0, 32 * g)))
                S.emit("pe", fns, reads=["Sb", f"Wst{l}"], writes=[pk])
                S.emit("act", lambda e, m=m, pb=pb: e.activation(out=cbf[:, m, 0:T], in_=pb[:, 0:T], func=AF.Identity,
                                                                 bias=cbc[l][:, m:m + 1], scale=1.0),
                       reads=[pk, f"cbc{l}"], writes=[f"cbf{m}"])
                S.emit("act", lambda e, m=m, pb=pb: e.activation(out=csq[:, m, 0:T], in_=pb[:, 0:T], func=AF.Square,
                                                                 bias=cbc[l][:, m:m + 1], scale=1.0),
                       reads=[pk, f"cbc{l}"], writes=[f"csq{m}"])
                S.emit("act", lambda e, m=m, pb=pb: e.activation(out=cT[:, m, 0:T], in_=pb[:, 0:T], func=AF.Identity,
                                                                 bias=cbc[l][:, m:m + 1], scale=1.0),
                       reads=[pk, f"cbc{l}"], writes=[f"cT{m}"])
                if m >= 1:
                    stat_mm(m - 1)
            stat_mm(3)

            S.emit("act", lambda e: e.copy(out=mean_sb[:, 0:T], in_=pbm[:, 0:T]), reads=[pkm], writes=["mean_sb"])
            S.emit("act", lambda e: e.activation(out=nt_a[:, 0:T], in_=pbm[:, 0:T], func=AF.Square), reads=[pkm], writes=["nt_a"])
            S.emit("act", lambda e: e.activation(out=junk1[:, 0:1], in_=one1[:, 0:1], func=AF.Ln), reads=["one1"], writes=["junk1"])
            S.emit("dve", lambda e: e.scalar_tensor_tensor(out=nt_a[:, 0:T], in0=pbq[:, 0:T], scalar=LN_EPS, in1=nt_a[:, 0:T],
                                                           op0=ALU.add, op1=ALU.subtract), reads=[pkq, "nt_a"], writes=["nt_a"])
            tpa, tka = next_tmp()
            tpb, tkb = next_tmp()
            S.emit("act", lambda e: e.activation(out=tpa[:, 0:T], in_=nt_a[:, 0:T], func=AF.Ln), reads=["nt_a"], writes=[tka])
            S.emit("act", lambda e: e.activation(out=rstd_b[:, 0:T], in_=tpa[:, 0:T], func=AF.Exp, scale=-0.5), reads=[tka], writes=["rstd_b"])
            S.emit("act", lambda e: e.activation(out=tpb[:, 0:T], in_=nt_a[:, 0:T], func=AF.Identity, scale=-0.5), reads=["nt_a"], writes=[tkb])
            S.emit("dve", lambda e: e.tensor_tensor(out=tpa[:, 0:T], in0=rstd_b[:, 0:T], in1=rstd_b[:, 0:T], op=ALU.mult), reads=["rstd_b"], writes=[tka])
            S.emit("dve", lambda e: e.tensor_tensor(out=tpa[:, 0:T], in0=tpa[:, 0:T], in1=tpb[:, 0:T], op=ALU.mult), reads=[tka, tkb], writes=[tka])
            S.emit("dve", lambda e: e.scalar_tensor_tensor(out=rstd_b[:, 0:T], in0=tpa[:, 0:T], scalar=1.5, in1=rstd_b[:, 0:T],
                                                           op0=ALU.add, op1=ALU.mult), reads=[tka, "rstd_b"], writes=["rstd_b"])
            def c_ga(m, pb, pk):
                S.emit("act", lambda e: e.activation(out=tTt[:, m, 0:T], in_=pb[:, 0:T], func=AF.Silu), reads=[pk], writes=[f"tT{m}"])
            fm_block(c_ga)

            def c_u(m, pb, pk):
                tp, tk = next_tmp()
                S.emit("act", lambda e: e.activation(out=tp[:, 0:T], in_=pb[:, 0:T], func=AF.Gelu), reads=[pk], writes=[tk])
                S.emit("pool", lambda e: e.tensor_tensor(out=tTt[:, m, 0:T], in0=tTt[:, m, 0:T], in1=tp[:, 0:T], op=ALU.mult),
                       reads=[tk, f"tT{m}"], writes=[f"tT{m}"])
            fm_block(c_u)

            for m in range(4):
                S.emit("dve", lambda e, m=m: e.tensor_tensor(out=cT[:, m, 0:T], in0=cT[:, m, 0:T], in1=mean_sb[:, 0:T], op=ALU.subtract),
                       reads=["mean_sb", f"cT{m}"], writes=[f"cT{m}"])
                S.emit("dve", lambda e, m=m: e.tensor_tensor(out=cT[:, m, 0:T], in0=cT[:, m, 0:T], in1=rstd_b[:, 0:T], op=ALU.mult),
                       reads=["rstd_b", f"cT{m}"], writes=[f"cT{m}"])
            for m in range(4):
                S.emit("act", lambda e, m=m: e.activation(out=cT[:, m, 0:T], in_=cT[:, m, 0:T], func=AF.Silu,
                                                          scale=bgc[l][:, m:m + 1], bias=bbc[l][:, m:m + 1]),
                       reads=[f"cT{m}", f"bgc{l}", f"bbc{l}"], writes=[f"cT{m}"])

            for m in range(4):
                pb, pk = next_bank()
                fns = []
                if sample:
                    for s, (c0, n) in enumerate(segs):
                        for hh in range(2):
                            h = 2 * m + hh
                            fns.append(lambda e, s=s, c0=c0, n=n, hh=hh, h=h, pb=pb: e.matmul(
                                pb[64 * hh:64 * hh + 64, c0:c0 + n], lhsT=vn[0:64, 0, 64 * h:64 * h + 64],
                                rhs=WmTs[l][0:64, s, h, 0:n], start=True, stop=True, tile_position=(0, 64 * hh)))
                    rk = ["vn0", f"WmTs{l}_0", f"WmTs{l}_1"]
                    lc = TS
                else:
                    for j in range(nj):
                        for hh in range(2):
                            h = 2 * m + hh
                            fns.append(lambda e, j=j, hh=hh, h=h, pb=pb: e.matmul(
                                pb[64 * hh:64 * hh + 64, 128 * j:128 * j + 128], lhsT=vn[:, j, 64 * h:64 * h + 64],
                                rhs=WmT[l][:, h, :], start=True, stop=True, tile_position=(0, 64 * hh)))
                    rk = [f"vn{j}" for j in range(nj)] + [f"WmT{l}"]
                    lc = 128
                S.emit("pe", fns, reads=rk, writes=[pk])
                nch = T // lc
                tp, tk = next_tmp()
                S.emit("dve", lambda e, m=m, pb=pb, tp=tp, nch=nch, lc=lc: e.scalar_tensor_tensor(
                    out=tp[:, 0:T].rearrange("p (c q) -> p c q", q=lc), in0=pb[:, 0:T].rearrange("p (c q) -> p c q", q=lc),
                    scalar=agc[l][:, m:m + 1], in1=Bt[l][:, m, 0:lc].unsqueeze(1).broadcast_to([128, nch, lc]),
                    op0=ALU.mult, op1=ALU.add), reads=[pk, f"agc{l}", f"Bt{l}_0", f"Bt{l}_1"], writes=[tk])
                if m < 3:
                    S.emit("pool", lambda e, m=m, tp=tp: e.tensor_tensor(out=yT[:, m, 0:T], in0=tp[:, 0:T], in1=tTt[:, m, 0:T], op=ALU.mult),
                           reads=[tk, f"tT{m}"], writes=[f"yT{m}"])
                else:
                    ya_last = (tp, tk)

            tp3, tk3 = ya_last
            S.emit("dve", lambda e, tp3=tp3: e.tensor_tensor(out=yT[:, 3, 0:T], in0=tp3[:, 0:T], in1=tTt[:, 3, 0:T], op=ALU.mult),
                   reads=[tk3, "tT3"], writes=["yT3"])
            for m in range(4):
                S.emit("dve", lambda e, m=m: e.tensor_tensor(out=yT[:, 4 + m, 0:T], in0=cT[:, m, 0:T], in1=sgb[:, m, 0:T], op=ALU.mult),
                       reads=[f"cT{m}", f"sgb{m}"], writes=[f"yT{4 + m}"])

            yTk = [f"yT{k}" for k in range(8)]
            wts = [w_block(), w_block(prefetch=False)]

            def wo_front(j):
                xr, xkey = xk[j]
                for n2 in range(2):
                    wt, wk = wts[n2]
                    pb, pk = next_bank()
                    S.emit("pe", [(lambda e, k=k, j=j, pb=pb, wt=wt: e.matmul(
                        pb[:PT, :], lhsT=yT[:, k, 128 * j:128 * j + PT], rhs=wt[:, k, :], start=(k == 0), stop=(k == 7)))
                        for k in range(8)], reads=[wk] + yTk, writes=[pk])
                    S.emit("dve", lambda e, xr=xr, pb=pb, n2=n2, j=j: e.scalar_tensor_tensor(
                        out=xr[:PT, 512 * n2:512 * n2 + 512], in0=xr[:PT, 512 * n2:512 * n2 + 512], scalar=ALPHA, in1=pb[:PT, :],
                        op0=ALU.mult, op1=ALU.add, accum_out=pst[:PT, j, 0, n2:n2 + 1]), reads=[pk, xkey], writes=[xkey, f"pst{j}_0{n2}"])
                    tp, tk = next_tmp()
                    S.emit("act", lambda e, xr=xr, tp=tp, n2=n2, j=j: e.activation(
                        out=tp[:PT, :], in_=xr[:PT, 512 * n2:512 * n2 + 512], func=AF.Square, accum_out=pst[:PT, j, 1, n2:n2 + 1]),
                        reads=[xkey], writes=[tk, f"pst{j}_1{n2}"])
                pk4 = [f"pst{j}_00", f"pst{j}_01", f"pst{j}_10", f"pst{j}_11"]
                S.emit("pool", lambda e, j=j: e.tensor_tensor(out=psc[:PT, j, 0:2], in0=pst[:PT, j, :, 0], in1=pst[:PT, j, :, 1], op=ALU.add),
                       reads=pk4, writes=[f"psc{j}"])
                S.emit("pool", lambda e, j=j: e.tensor_scalar(out=psc[:PT, j, 0:2], in0=psc[:PT, j, 0:2], scalar1=1.0 / D, scalar2=None, op0=ALU.mult),
                       reads=[f"psc{j}"], writes=[f"psc{j}"])
                S.emit("pool", lambda e, j=j: e.tensor_tensor(out=psc[:PT, j, 2:3], in0=psc[:PT, j, 0:1], in1=psc[:PT, j, 0:1], op=ALU.mult),
                       reads=[f"psc{j}"], writes=[f"psc{j}"])
                S.emit("pool", lambda e, j=j: e.tensor_scalar(out=psc[:PT, j, 3:4], in0=psc[:PT, j, 1:2], scalar1=LN_EPS, scalar2=None, op0=ALU.add),
                       reads=[f"psc{j}"], writes=[f"psc{j}"])
                S.emit("pool", lambda e, j=j: e.tensor_tensor(out=psc[:PT, j, 3:4], in0=psc[:PT, j, 3:4], in1=psc[:PT, j, 2:3], op=ALU.subtract),
                       reads=[f"psc{j}"], writes=[f"psc{j}"])
                rsqrt_pool(sm[1][:PT, j:j + 1], psc[:PT, j, 3:4], [f"psc{j}"], f"pl1_{j}", (PT, 1))

            def wo_back(j):
                xr, xkey = xk[j]
                S.emit("dve", lambda e, xr=xr, j=j: e.scalar_tensor_tensor(
                    out=xr[:PT, :], in0=xr[:PT, :], scalar=psc[:PT, j, 0:1], in1=pg[l][:PT, :], op0=ALU.subtract, op1=ALU.mult),
                    reads=[xkey, f"psc{j}", f"pg{l}"], writes=[xkey])
                S.emit("dve", lambda e, xr=xr, j=j: e.scalar_tensor_tensor(
                    out=xr[:PT, :], in0=xr[:PT, :], scalar=sm[1][:PT, j:j + 1], in1=pbb[l][:PT, :], op0=ALU.mult, op1=ALU.add),
                    reads=[xkey, f"pl1_{j}", f"pbb{l}"], writes=[xkey])
                if y_out is not None:
                    S.emit("sp", lambda e, xr=xr, j=j: e.dma_start(out=y_out[j], in_=xr[:PT, :]), reads=[xkey], dma="d_o" + xkey[-1])

            for j in range(nj):
                wo_front(j)
                if j >= 1:
                    wo_back(j - 1)
            wo_back(nj - 1)

        tiles_per_seq = seq_len // TT
        n_tiles = n_seq * tiles_per_seq
        n_tl = n_tiles + (0 if "nosample" in _DBG else 1)
        for t in range(n_tl):
            for l in range(DEPTH):
                plan.extend((l, b, t == 0) for b in range(8))

        def chunk_idx(t, j):
            return (4 * t + j) % 7

        def emit_xload(t, js):
            b = t // tiles_per_seq
            ti = t % tiles_per_seq
            for j in js:
                idx = chunk_idx(t, j)
                S.emit("sp", lambda e, idx=idx, b=b, ti=ti, j=j: e.dma_start(
                    out=xres[idx][:, :], in_=xp[b, ti * TT + 128 * j:ti * TT + 128 * j + 128, :]),
                    writes=[f"xres{idx}"], dma=f"d_x{idx}")

        emit_xload(0, [0, 1, 2, 3])
        sidx = chunk_idx(n_tiles, 0)
        for t in range(n_tiles):
            b = t // tiles_per_seq
            ti = t % tiles_per_seq
            xk = [(xres[chunk_idx(t, j)], f"xres{chunk_idx(t, j)}") for j in range(4)]
            if t + 1 < n_tiles:
                emit_xload(t + 1, [0, 1, 2])
            else:
                S.emit("sp", lambda e: e.dma_start(out=xres[sidx][0:n_samp * DEC_SEQ, :], in_=xs.rearrange("b t d -> (b t) d")),
                       writes=[f"xres{sidx}"], dma=f"d_x{sidx}")
                for l in range(DEPTH):
                    c = chunk_idx(n_tiles, 1 + l)
                    av_chunk[l] = c
                    S.emit("sp", [lambda e, c=c, l=l: e.dma_start(out=xres[c][:, 0:512], in_=a_ln_g[l].partition_broadcast(128)),
                                  lambda e, c=c, l=l: e.dma_start(out=xres[c][:, 512:1024], in_=a_ln_b[l].partition_broadcast(128))],
                           writes=[f"xres{c}"], dma=f"d_x{c}")
            for l in range(DEPTH):
                tile_layer(l, TT, [(0, TT)], xk, ti == 0, ti == tiles_per_seq - 1, False, None,
                           [ncp[l, b]],
                           [yp[b, ti * TT + 128 * j:ti * TT + 128 * j + 128, :] for j in range(4)] if l == DEPTH - 1 else None,
                           None)
            if t + 1 < n_tiles:
                emit_xload(t + 1, [3])

        xr0 = (xres[sidx], f"xres{sidx}")
        ssegs = [(DEC_SEQ * s, DEC_SEQ) for s in range(n_samp)]
        for l in range(DEPTH if "nosample" not in _DBG else 0):
            tile_layer(l, n_samp * DEC_SEQ, ssegs, [xr0], False, True, True, 0,
                       [ncs[l, s] for s in range(n_samp)],
                       [ys.rearrange("b t d -> (b t) d")] if l == DEPTH - 1 else None,
                       [nav[l, s] for s in range(n_samp)])

        S.wait_all("sp", ["d_o0", "d_o1", "d_o2", "d_o3", "d_o4", "d_o5", "d_o6", "d_x5", "d_x6", "d_oc", "d_ov", "d_ws0", "d_ws1", "d_ws2", "d_ws3", "d_m", "d_mb", "d_m3a", "d_m3b", "d_m2", "d_s", "d_x0", "d_x1", "d_x2", "d_x3", "d_x4", "d_x5", "d_x6", "d_o5", "d_o6", "d_c", "d_pm"])

        with nc.allow_non_contiguous_dma(reason="tiny parameter gathers in the prologue"):
            with nc.Block() as block:
                @block.tensor
                def _(e):
                    S.replay("pe", e)

                @block.scalar
                def _(e):
                    S.replay("act", e)

                @block.vector
                def _(e):
                    S.replay("dve", e)

                @block.gpsimd
                def _(e):
                    S.replay("pool", e)

                @block.sync
                def _(e):
                    S.replay("sp", e)
    return nc


_CONSTS = None


def _consts():
    global _CONSTS
    if _CONSTS is None:
        k = np.arange(128)
        _CONSTS = {
            "c_ident": np.eye(128, dtype=np.float32),
            "c_tril": (k[:, None] <= k[None, :]).astype(np.float32),
            "c_i32": (k[:, None] % 32 == np.arange(32)[None, :]).astype(np.float32),
        }
    return _CONSTS


_NC_CACHE = {}


def kernel(x_prompt, x_sample, cache_conv, w_in, a_ln_g, a_ln_b, a_ws, a_bias, b_conv_w, b_conv_b,
           b_ln_g, b_ln_b, w_out, post_ln_g, post_ln_b):
    f = lambda a: np.ascontiguousarray(np.asarray(a, dtype=np.float32))
    x_prompt, x_sample, cache_conv = f(x_prompt), f(x_sample), f(cache_conv)
    shared = {"w_in": f(w_in), "w_out": f(w_out), "a_ln_g": f(a_ln_g), "a_ln_b": f(a_ln_b), "a_ws": f(a_ws),
              "a_bias": f(a_bias), "b_conv_w": f(b_conv_w), "b_conv_b": f(b_conv_b), "b_ln_g": f(b_ln_g),
              "b_ln_b": f(b_ln_b), "post_ln_g": f(post_ln_g), "post_ln_b": f(post_ln_b)}
    shared.update(_consts())
    nb, seq_len = x_prompt.shape[0], x_prompt.shape[1]
    n_seq = nb // NCORES
    n_samp = x_sample.shape[0] // NCORES
    key = (n_seq, seq_len, n_samp)
    if key not in _NC_CACHE:
        _NC_CACHE[key] = build_nc(n_seq, seq_len, n_samp)
    nc = _NC_CACHE[key]
    in_maps = []
    for c in range(NCORES):
        m = dict(shared)
        m["xp"] = np.ascontiguousarray(x_prompt[c * n_seq:(c + 1) * n_seq])
        m["xs"] = np.ascontiguousarray(x_sample[c * n_samp:(c + 1) * n_samp])
        m["cc"] = np.ascontiguousarray(cache_conv[:, c * n_samp:(c + 1) * n_samp])
        in_maps.append(m)
    res = run_bass_kernel_spmd(nc, in_maps, core_ids=list(range(NCORES)))
    R = res.results
    y_p = np.concatenate([r["yp"] for r in R], axis=0)
    y_s = np.concatenate([r["ys"] for r in R], axis=0)
    ncp = np.concatenate([r["ncp"] for r in R], axis=1)
    ncs = np.concatenate([r["ncs"] for r in R], axis=1)
    nav = np.concatenate([r["nav"] for r in R], axis=1)
    return (y_p.astype(np.float32), y_s.astype(np.float32), ncp.astype(np.float32), ncs.astype(np.float32),
            nav.astype(np.float32))
```
